# Optimizing a Trainium2 kernel written in Bass

```python
import math
import jax, jax.numpy as jnp
from jax import lax
import numpy as np


D_MODEL = 1024
BATCH = 2
SEQ = 8192
DEPTH = 4
DEC_BATCH = 128
DEC_SEQ = 8
PAST_LEN = 8192
PAGE_SIZE = 128

N_A_LAYERS = DEPTH // 2
N_B_LAYERS = DEPTH - N_A_LAYERS
FFN_DIM = 2816
CONV_DIM = D_MODEL
CONV_WIDTH = 3
N_HEADS = 16
N_KV_HEADS = 4
HEAD_DIM = 64
GROUP = N_HEADS // N_KV_HEADS
ATTN_DIM = N_HEADS * HEAD_DIM
WINDOW = 128
REL_BUCKETS = 32
REL_MAX_DIST = 128
MEM_TOKENS = 256
MEM_HEADS = 4
MEM_HEAD_DIM = 128
MEM_DIM = MEM_HEADS * MEM_HEAD_DIM
RMS_EPS = 1e-5

kernel_name = 'yoco_shortconv_swa_sink_macaron_memory_step'


def rms_norm(x, g):
    xf = x.astype(jnp.float32)
    y = xf * lax.rsqrt(jnp.mean(xf * xf, axis=-1, keepdims=True) + RMS_EPS)
    return (y * g.astype(jnp.float32)).astype(x.dtype)


def swiglu(h, wg, wu, wd):
    return (jax.nn.silu(h @ wg) * (h @ wu)) @ wd


def t5_bucket(dist):
    n = jnp.maximum(dist, 0)
    exact = REL_BUCKETS // 2
    nf = jnp.maximum(n, 1).astype(jnp.float32)
    large = exact + (jnp.log(nf / exact) / math.log(REL_MAX_DIST / exact)
                     * (REL_BUCKETS - exact)).astype(jnp.int32)
    large = jnp.minimum(large, REL_BUCKETS - 1)
    return jnp.where(n < exact, n, large)


def short_conv(u, prefix, w):
    t = u.shape[1]
    up = jnp.concatenate([prefix.astype(u.dtype), u], axis=1)
    out = sum(w[i].astype(u.dtype) * up[:, i:i + t] for i in range(CONV_WIDTH))
    return out, up[:, -(CONV_WIDTH - 1):]


def memory_kv(mem, g, w):
    b, m, _ = mem.shape
    k, v = jnp.split(rms_norm(mem, g) @ w, 2, axis=-1)
    return (k.reshape(b, m, MEM_HEADS, MEM_HEAD_DIM), v.reshape(b, m, MEM_HEADS, MEM_HEAD_DIM))


def memory_attention(q, mk, mv):
    b, t = q.shape[:2]
    s = jnp.einsum('bthd,bmhd->bhtm', q, mk).astype(jnp.float32) * (MEM_HEAD_DIM ** -0.5)
    p = jax.nn.softmax(s, axis=-1).astype(mv.dtype)
    return jnp.einsum('bhtm,bmhd->bthd', p, mv).reshape(b, t, MEM_DIM)


def sink_window_attention(q, k, v, dist, valid, rel_bias, sinks):
    tq, tk = dist.shape
    s = jnp.einsum('bnqhgd,bnkhd->bnhgqk', q, k).astype(jnp.float32) * (HEAD_DIM ** -0.5)
    bias = rel_bias.astype(jnp.float32)[t5_bucket(dist)]
    bias = bias.reshape(tq, tk, N_KV_HEADS, GROUP).transpose(2, 3, 0, 1)
    mask = valid[None, :, None, None] & ((dist >= 0) & (dist < WINDOW))
    s = jnp.where(mask, s + bias, -jnp.inf)
    sink = sinks.astype(jnp.float32).reshape(N_KV_HEADS, GROUP)[:, :, None, None]
    m = jnp.maximum(jnp.max(s, axis=-1, keepdims=True), sink)
    p = jnp.exp(s - m)
    p = p / (jnp.sum(p, axis=-1, keepdims=True) + jnp.exp(sink - m))
    return jnp.einsum('bnhgqk,bnkhd->bnqhgd', p.astype(v.dtype), v)


def trunk(x, conv_prefix, swa_past_k, swa_past_v, mem_k, mem_v, W):
    b, t, _ = x.shape
    conv_states = []
    for l in range(DEPTH):
        if l == N_A_LAYERS:
            k_new, v_new = jnp.split(rms_norm(x, W['kv_norm']) @ W['w_kv'], 2, axis=-1)
            k_new = k_new.reshape(b, t, N_KV_HEADS, HEAD_DIM)
            v_new = v_new.reshape(b, t, N_KV_HEADS, HEAD_DIM)
            if swa_past_k is None:
                nb = t // WINDOW
                q_block = WINDOW
                kb = k_new.reshape(b, nb, WINDOW, N_KV_HEADS, HEAD_DIM)
                vb = v_new.reshape(b, nb, WINDOW, N_KV_HEADS, HEAD_DIM)
                kprev = jnp.concatenate([jnp.zeros_like(kb[:, :1]), kb[:, :-1]], axis=1)
                vprev = jnp.concatenate([jnp.zeros_like(vb[:, :1]), vb[:, :-1]], axis=1)
                keys = jnp.concatenate([kprev, kb], axis=2)
                vals = jnp.concatenate([vprev, vb], axis=2)
                valid = (jnp.arange(nb)[:, None, None] > 0) | (jnp.arange(2 * WINDOW)[None, None, :] >= WINDOW)
                swa_k_state = k_new[:, -WINDOW:]
                swa_v_state = v_new[:, -WINDOW:]
            else:
                q_block = t
                keys_flat = jnp.concatenate([swa_past_k.astype(k_new.dtype), k_new], axis=1)
                vals_flat = jnp.concatenate([swa_past_v.astype(v_new.dtype), v_new], axis=1)
                keys = keys_flat[:, None]
                vals = vals_flat[:, None]
                valid = jnp.ones((1, 1, WINDOW + t), dtype=bool)
                swa_k_state = keys_flat[:, -WINDOW:]
                swa_v_state = vals_flat[:, -WINDOW:]
            dist = WINDOW + jnp.arange(q_block)[:, None] - jnp.arange(keys.shape[2])[None, :]
        x = x + 0.5 * swiglu(rms_norm(x, W['ffn1_norm'][l]), W['ffn1_wg'][l], W['ffn1_wu'][l], W['ffn1_wd'][l])
        h = rms_norm(x, W['mix_norm'][l])
        if l < N_A_LAYERS:
            z = h @ W['w_in_a'][l]
            b_g, c_g, xin, qm = jnp.split(z, [CONV_DIM, 2 * CONV_DIM, 3 * CONV_DIM], axis=-1)
            conv_out, st = short_conv(c_g * xin, conv_prefix[l], W['conv_w'][l])
            conv_states.append(st)
            y_tok = b_g * conv_out
            w_out = W['w_out_a'][l]
        else:
            j = l - N_A_LAYERS
            z = h @ W['w_in_b'][j]
            qa, qm = jnp.split(z, [ATTN_DIM], axis=-1)
            qa = qa.reshape(b, t // q_block, q_block, N_KV_HEADS, GROUP, HEAD_DIM)
            o = sink_window_attention(qa, keys, vals, dist, valid, W['rel_bias'], W['attn_sinks'][j])
            y_tok = o.reshape(b, t, ATTN_DIM)
            w_out = W['w_out_b'][j]
        y_mem = memory_attention(qm.reshape(b, t, MEM_HEADS, MEM_HEAD_DIM), mem_k[l], mem_v[l])
        x = x + jnp.concatenate([y_tok, y_mem], axis=-1) @ w_out
        x = x + 0.5 * swiglu(rms_norm(x, W['ffn2_norm'][l]), W['ffn2_wg'][l], W['ffn2_wu'][l], W['ffn2_wd'][l])
    return rms_norm(x, W['final_norm']), jnp.stack(conv_states), swa_k_state, swa_v_state


def setup_inputs(seed: int = 0) -> dict:
    key = jax.random.key(seed)
    ks = iter(jax.random.split(key, 40))
    f32 = jnp.float32
    D = D_MODEL

    def nrm(shape, scale):
        return jax.random.normal(next(ks), shape, f32) * scale

    def gain(shape):
        return 1.0 + 0.05 * jax.random.normal(next(ks), shape, f32)

    return {
        'x_prompt': nrm((BATCH, SEQ, D), 1.0),
        'x_sample': nrm((DEC_BATCH, DEC_SEQ, D), 1.0),
        'state_conv': nrm((N_A_LAYERS, DEC_BATCH, CONV_WIDTH - 1, CONV_DIM), 1.0),
        'cache_swa_k': nrm((DEC_BATCH, WINDOW, N_KV_HEADS, HEAD_DIM), 1.0),
        'cache_swa_v': nrm((DEC_BATCH, WINDOW, N_KV_HEADS, HEAD_DIM), 1.0),
        'cache_mem_k': nrm((DEPTH, DEC_BATCH, MEM_TOKENS, MEM_HEADS, MEM_HEAD_DIM), 1.0),
        'cache_mem_v': nrm((DEPTH, DEC_BATCH, MEM_TOKENS, MEM_HEADS, MEM_HEAD_DIM), 1.0),
        'mem_prompt': nrm((BATCH, MEM_TOKENS, D), 1.0),
        'ffn1_norm': gain((DEPTH, D)),
        'ffn1_wg': nrm((DEPTH, D, FFN_DIM), D ** -0.5),
        'ffn1_wu': nrm((DEPTH, D, FFN_DIM), D ** -0.5),
        'ffn1_wd': nrm((DEPTH, FFN_DIM, D), FFN_DIM ** -0.5),
        'mix_norm': gain((DEPTH, D)),
        'w_in_a': nrm((N_A_LAYERS, D, 3 * CONV_DIM + MEM_DIM), D ** -0.5),
        'conv_w': nrm((N_A_LAYERS, CONV_WIDTH, CONV_DIM), CONV_WIDTH ** -0.5),
        'w_out_a': nrm((N_A_LAYERS, CONV_DIM + MEM_DIM, D), (CONV_DIM + MEM_DIM) ** -0.5),
        'kv_norm': gain((D,)),
        'w_kv': nrm((D, 2 * N_KV_HEADS * HEAD_DIM), D ** -0.5),
        'w_in_b': nrm((N_B_LAYERS, D, ATTN_DIM + MEM_DIM), D ** -0.5),
        'attn_sinks': nrm((N_B_LAYERS, N_HEADS), 1.0),
        'rel_bias': nrm((REL_BUCKETS, N_HEADS), 0.5),
        'w_out_b': nrm((N_B_LAYERS, ATTN_DIM + MEM_DIM, D), (ATTN_DIM + MEM_DIM) ** -0.5),
        'mem_norm': gain((DEPTH, D)),
        'w_mem_kv': nrm((DEPTH, D, 2 * MEM_DIM), D ** -0.5),
        'ffn2_norm': gain((DEPTH, D)),
        'ffn2_wg': nrm((DEPTH, D, FFN_DIM), D ** -0.5),
        'ffn2_wu': nrm((DEPTH, D, FFN_DIM), D ** -0.5),
        'ffn2_wd': nrm((DEPTH, FFN_DIM, D), FFN_DIM ** -0.5),
        'final_norm': gain((D,)),
    }


def reference(x_prompt, x_sample, state_conv, cache_swa_k, cache_swa_v, cache_mem_k, cache_mem_v,
              mem_prompt, ffn1_norm, ffn1_wg, ffn1_wu, ffn1_wd, mix_norm, w_in_a, conv_w, w_out_a,
              kv_norm, w_kv, w_in_b, attn_sinks, rel_bias, w_out_b, mem_norm, w_mem_kv,
              ffn2_norm, ffn2_wg, ffn2_wu, ffn2_wd, final_norm):
    W = dict(ffn1_norm=ffn1_norm, ffn1_wg=ffn1_wg, ffn1_wu=ffn1_wu, ffn1_wd=ffn1_wd,
             mix_norm=mix_norm, w_in_a=w_in_a, conv_w=conv_w, w_out_a=w_out_a,
             kv_norm=kv_norm, w_kv=w_kv, w_in_b=w_in_b, attn_sinks=attn_sinks,
             rel_bias=rel_bias, w_out_b=w_out_b,
             ffn2_norm=ffn2_norm, ffn2_wg=ffn2_wg, ffn2_wu=ffn2_wu, ffn2_wd=ffn2_wd,
             final_norm=final_norm)
    mk_list, mv_list = [], []
    for l in range(DEPTH):
        mk, mv = memory_kv(mem_prompt, mem_norm[l], w_mem_kv[l])
        mk_list.append(mk)
        mv_list.append(mv)
    mem_k_prompt = jnp.stack(mk_list)
    mem_v_prompt = jnp.stack(mv_list)
    conv_zero = jnp.zeros((N_A_LAYERS, x_prompt.shape[0], CONV_WIDTH - 1, CONV_DIM), x_prompt.dtype)
    y_prompt, conv_state_prompt, swa_k_prompt, swa_v_prompt = trunk(
        x_prompt, conv_zero, None, None, mem_k_prompt, mem_v_prompt, W)
    y_sample, conv_state_sample, swa_k_sample, swa_v_sample = trunk(
        x_sample, state_conv, cache_swa_k, cache_swa_v, cache_mem_k, cache_mem_v, W)
    return (y_prompt, y_sample, conv_state_prompt, conv_state_sample,
            swa_k_prompt, swa_v_prompt, swa_k_sample, swa_v_sample,
            mem_k_prompt, mem_v_prompt)
```

```python
import math
from contextlib import ExitStack

import numpy as np
import concourse.bass as bass
import concourse.mybir as mybir
from concourse.bass_utils import run_bass_kernel_spmd

F32 = mybir.dt.float32
BF16 = mybir.dt.bfloat16
AF = mybir.ActivationFunctionType
ALU = mybir.AluOpType

D = 1024
KC = 8
FF = 2816
FC = 22
DEPTH = 4
NPASS = 2
HALO = 132
MAIN = 1024
SAMP = 64
NSEQ = 8
NT = HALO + MAIN + SAMP
CM0 = HALO
CS0 = HALO + MAIN
NB = NT - HALO
ABND = [4] + [132 + 128 * i for i in range(9)] + [NT]
BND = sorted(set([0] + ABND + [407, 814, 495, 858, 410, 815]))
NBLK = len(BND) - 1
RMS_EPS = 1e-5
NW = 5
WSLOT = 2816
SEM_ROLL = 30000
DEFER_NORM_TAIL = False
ENGS = ("pe", "act", "dve", "pool", "sp")


class Buf:
    __slots__ = ("w", "r", "x")

    def __init__(self, excl=False):
        self.w = None
        self.r = []
        self.x = excl


class Chan:
    __slots__ = ("sem", "count", "last")

    def __init__(self, sem):
        self.sem = sem
        self.count = 0
        self.last = None


class Op:
    __slots__ = ("eng", "fn", "deps", "chan", "sig", "need")

    def __init__(self, eng, fn, chan):
        self.eng = eng
        self.fn = fn
        self.deps = []
        self.chan = chan
        self.sig = None
        self.need = False


class Sched:
    def __init__(self, nc, stack):
        self.nc = nc
        self.stack = stack
        self.ops = {e: [] for e in ENGS}
        self.all = []
        self.chans = []

    def new_chan(self, name):
        sem = self.stack.enter_context(self.nc.semaphore(name))
        c = Chan(sem)
        self.chans.append(c)
        return c

    def op(self, eng, fn, reads=(), writes=(), chan=None, join=False):
        o = Op(eng, fn, chan)
        deps = {}
        xr = [b for b in reads if b.x]
        if xr:
            reads = [b for b in reads if not b.x]
            writes = list(writes) + xr
        for b in reads:
            if b.w is not None:
                deps[id(b.w)] = b.w
        for b in writes:
            if b.w is not None:
                deps[id(b.w)] = b.w
            for r in b.r:
                deps[id(r)] = r
        if chan is not None and chan.last is not None:
            if join:
                deps.pop(id(chan.last), None)
            else:
                deps[id(chan.last)] = chan.last
        for b in reads:
            b.r.append(o)
        for b in writes:
            b.w = o
            b.r = []
        if chan is not None:
            chan.last = o
        dl = []
        for d in deps.values():
            if d is o:
                continue
            if d.chan is None and chan is None and d.eng == "pe" and eng == "pe":
                continue
            d.need = True
            dl.append(d)
        o.deps = dl
        self.ops[eng].append(o)
        self.all.append(o)
        return o

    def emit(self):
        nc = self.nc
        for o in self.all:
            if o.chan is not None:
                o.chan.count += 16
                o.sig = (o.chan.sem, o.chan.count, 16)
        for e in ENGS:
            cur = None
            cnt = 0
            k = 0
            for o in self.ops[e]:
                if o.chan is not None:
                    continue
                if o.need:
                    if cur is None or cnt >= SEM_ROLL:
                        cur = self.stack.enter_context(nc.semaphore("s_%s_%d" % (e, k)))
                        k += 1
                        cnt = 0
                    cnt += 1
                    o.sig = (cur, cnt, 1)
        finals = [(c.sem, c.count) for c in self.chans if c.count > 0]
        ops = self.ops

        def run(e, eng):
            waited = {}
            for o in ops[e]:
                for d in o.deps:
                    sem, val, _ = d.sig
                    if waited.get(sem.num, 0) < val:
                        eng.wait_ge(sem, val)
                        waited[sem.num] = val
                ins = o.fn(eng)
                if o.sig is not None:
                    ins.then_inc(o.sig[0], o.sig[2])
            if e == "sp":
                for sem, val in finals:
                    if waited.get(sem.num, 0) < val:
                        eng.wait_ge(sem, val)

        with nc.Block() as block:
            @block.tensor
            def _(eng):
                run("pe", eng)

            @block.scalar
            def _(eng):
                run("act", eng)

            @block.vector
            def _(eng):
                run("dve", eng)

            @block.gpsimd
            def _(eng):
                run("pool", eng)

            @block.sync
            def _(eng):
                run("sp", eng)


def tiles(c0, c1, mx=512):
    n = c1 - c0
    k = (n + mx - 1) // mx
    base = n // k
    rem = n % k
    out = []
    c = c0
    for i in range(k):
        w = base + (1 if i < rem else 0)
        out.append((c, c + w))
        c += w
    return out


def blks(c0, c1):
    return [i for i in range(NBLK) if BND[i] < c1 and BND[i + 1] > c0]


class _Stop(Exception):
    pass


class Prog:
    def __init__(self, stop=None):
        self.stop = stop
        self.nc = nc = bass.Bass("TRN2", target_bir_lowering=False)
        self.st = ExitStack()
        self.S = Sched(nc, self.st)
        self.B = {}
        self.dr = {}

    def bb(self, *key):
        b = self.B.get(key)
        if b is None:
            b = self.B[key] = Buf(excl=(key[0] == "ps"))
        return b

    def bl(self, name, chs, c0, c1):
        return [self.bb(name, ch, b) for ch in chs for b in blks(c0, c1)]

    def din(self, name, shape, dt=F32):
        self.dr[name] = self.nc.dram_tensor(name, list(shape), dt, kind="ExternalInput").ap()
        return self.dr[name]

    def dout(self, name, shape, dt=F32):
        self.dr[name] = self.nc.dram_tensor(name, list(shape), dt, kind="ExternalOutput").ap()
        return self.dr[name]

    def dscr(self, name, shape, dt):
        self.dr[name] = self.nc.dram_tensor(name, list(shape), dt, kind="Internal").ap()
        return self.dr[name]

    def sb(self, name, shape, dt):
        return self.st.enter_context(self.nc.sbuf_tensor(name, list(shape), dt))

    def ring(self, name, n):
        return {"i": 0, "n": n, "name": name}

    def nxt(self, rg):
        i = rg["i"]
        rg["i"] = (i + 1) % rg["n"]
        return i

    def pe(self, lst, reads, writes):
        lst = list(lst)

        def fn(e):
            ins = None
            for (o, l, r, s, t) in lst:
                ins = e.matmul(o, lhsT=l, rhs=r, start=s, stop=t)
            return ins
        self.S.op("pe", fn, reads, writes)

    def pe_tr(self, lst, reads, writes):
        lst = list(lst)

        def fn(e):
            ins = None
            for (o, i, idn) in lst:
                ins = e.transpose(o, i, idn)
            return ins
        self.S.op("pe", fn, reads, writes)

    def act(self, out, in_, func, reads, writes, scale=None, bias=None):
        kw = {}
        if scale is not None:
            kw["scale"] = scale
        if bias is not None:
            kw["bias"] = bias
        self.S.op("act", lambda e: e.activation(out=out, in_=in_, func=func, **kw), reads, writes)

    def tt(self, out, in0, in1, op, reads, writes):
        self.S.op("dve", lambda e: e.tensor_tensor(out=out, in0=in0, in1=in1, op=op), reads, writes)

    def stt(self, out, in0, scalar, in1, op0, op1, reads, writes):
        self.S.op("dve", lambda e: e.scalar_tensor_tensor(out=out, in0=in0, scalar=scalar, in1=in1, op0=op0, op1=op1), reads, writes)

    def ts(self, out, in0, s1, op0, reads, writes):
        self.S.op("dve", lambda e: e.tensor_scalar(out=out, in0=in0, scalar1=s1, scalar2=None, op0=op0), reads, writes)

    def cp(self, out, in_, reads, writes, eng="dve"):
        if eng == "dve":
            self.S.op("dve", lambda e: e.tensor_copy(out=out, in_=in_), reads, writes)
        else:
            self.S.op("act", lambda e: e.copy(out=out, in_=in_), reads, writes)

    def dma(self, q, out, in_, reads, writes, chan, join=False, **kw):
        self.S.op(q, lambda e: e.dma_start(out=out, in_=in_, **kw), reads, writes, chan=chan, join=join)

    def ps_next(self):
        i = self.nxt(self.psr)
        return self.ps[i], self.bb("ps", i)

    def t32_next(self):
        i = self.nxt(self.t32r)
        return self.t32[i], self.bb("t32", i)

    def t16_next(self):
        i = self.nxt(self.t16r)
        return self.t16[i], self.bb("t16", i)

    def stg_next(self):
        i = self.nxt(self.stgr)
        return self.stg[i], self.bb("stg", i), self.stgc[i]

    def wload(self, src, a, b):
        i = self.nxt(self.wr)
        v = self.wt[i][:, 0:a * b].rearrange("p (a b) -> p a b", a=a)
        bf = self.bb("w", i)
        self.dma("pool", v, src, [], [bf], self.wc[i])
        return v, bf

    def stage(self, name):
        if self.stop == name:
            raise _Stop()

    def build(self):
        try:
            self._build()
        except _Stop:
            x, h = self.x, self.h
            xd = self.dout("xdump", [128, KC, NT])
            hd = self.dout("hdump", [128, KC, NT], BF16)
            bd = self.dout("bigdump", [128, FC, NT], BF16)
            allb = lambda nm, n: [self.bb(nm, ch, b) for ch in range(n) for b in range(NBLK)]
            c1, c2, c3 = [self.S.new_chan("dbg%d" % i) for i in range(3)]
            self.dma("sp", xd, x[:], allb("x", KC), [], c1)
            self.dma("sp", hd, h[:], allb("h", KC), [], c2)
            self.dma("sp", bd, self.big, allb("big", FC) + [self.bb(k) for k in ("tmpc", "acc", "up", "us", "cs")], [], c3)
        self.S.emit()
        return self.nc

    def _build(self):
        nc = self.nc
        S = self.S
        xin = self.din("xin", [NPASS, NT, D])
        scv = self.din("scv", [NPASS, 2, 2 * NSEQ, D])
        ck = self.din("ck", [NPASS, NSEQ, 128, 256])
        cv = self.din("cv", [NPASS, NSEQ, 128, 256])
        cmk = self.din("cmk", [DEPTH, NPASS, NSEQ, 256, 512])
        cmv = self.din("cmv", [DEPTH, NPASS, NSEQ, 256, 512])
        mem = self.din("mem", [256, D])
        vecs = self.din("vecs", [24, D])
        w_f1g = self.din("f1g", [DEPTH, D, FF])
        w_f1u = self.din("f1u", [DEPTH, D, FF])
        w_f1d = self.din("f1d", [DEPTH, FF, D])
        w_f2g = self.din("f2g", [DEPTH, D, FF])
        w_f2u = self.din("f2u", [DEPTH, D, FF])
        w_f2d = self.din("f2d", [DEPTH, FF, D])
        w_ina = self.din("wina", [2, D, 3584])
        w_outa = self.din("wouta", [2, 1536, D])
        w_inb = self.din("winb", [2, D, 1536])
        w_outb = self.din("woutb", [2, 1536, D])
        w_kv = self.din("wkv", [D, 512])
        w_mkv = self.din("wmkv", [DEPTH, D, 1024])
        relb = self.din("relb", [32, 16])
        sinks = self.din("sinks", [2, 16])
        ident_d = self.din("ident", [128, 128])
        J_d = self.din("jmat", [128, 383])
        OH_d = self.din("ohot", [32, 128])
        BD_d = self.din("bdiag", [64, 64])
        hv_d = self.din("hv", [NPASS, 128, 64])

        yout = self.dout("yout", [NPASS, NB, D])
        cso = self.dout("cso", [NPASS, 2, 18, D])
        swk = self.dout("swk", [NPASS, 128, 256])
        swv = self.dout("swv", [NPASS, 128, 256])
        sks = self.dout("sks", [NPASS, NSEQ, 128, 256])
        svs = self.dout("svs", [NPASS, NSEQ, 128, 256])
        mko = self.dout("mko", [DEPTH, 256, 512])
        mvo = self.dout("mvo", [DEPTH, 256, 512])
        mkt_s = self.dscr("mkt_s", [DEPTH, 128, 1024], BF16)
        mv_s = self.dscr("mv_s", [DEPTH, 2, 128, 512], BF16)

        self.x = x = self.sb("x", [128, KC, NT], F32)
        self.h = h = self.sb("h", [128, KC, NT], BF16)
        bigT = self.sb("big", [128, FC * NT], BF16)
        self.big = big = bigT[:].rearrange("p (c t) -> p c t", c=FC)
        big32 = bigT.bitcast(F32)
        self.wt = [self.sb("w%d" % i, [128, WSLOT], BF16) for i in range(NW)]
        self.wc = [S.new_chan("wc%d" % i) for i in range(NW)]
        self.wr = self.ring("w", NW)
        self.KT = KT = self.sb("KT", [128, 4, NT - 4], BF16)
        self.V = V = self.sb("V", [128, 10, 256], BF16)
        self.E = E = self.sb("E", [128, 2, 16, 128], BF16)
        self.En = En = self.sb("En", [64, 2, 512], BF16)
        self.mKT = mKT = self.sb("mKT", [128, 4, 256], BF16)
        self.mV = mV = self.sb("mV", [128, 2, 512], BF16)
        kmc = [self.sb("kmc%d" % i, [128, 2, 512], BF16) for i in range(2)]
        vmc = [self.sb("vmc%d" % i, [128, 2, 512], BF16) for i in range(2)]
        kmT = self.sb("kmT", [128, 4, 2, 128], BF16)
        kc16 = [self.sb("kc16_%d" % i, [128, 4, 2, 64], BF16) for i in range(2)]
        kcT = [self.sb("kcT%d" % i, [128, 4, 128], BF16) for i in range(2)]
        vc16 = [self.sb("vc16_%d" % i, [128, 256], BF16) for i in range(2)]
        self.stg = [self.sb("stg%d" % i, [128, 1024], F32) for i in range(2)]
        self.stgc = [S.new_chan("stgc%d" % i) for i in range(2)]
        self.stgr = self.ring("stg", 2)
        self.t32 = [self.sb("t32_%d" % i, [128, 512], F32) for i in range(3)]
        self.t32r = self.ring("t32", 3)
        self.t16 = [self.sb("t16_%d" % i, [128, 512], BF16) for i in range(4)]
        self.t16r = self.ring("t16", 4)
        ident = self.sb("identf", [128, 128], F32)
        identb = self.sb("identb", [128, 128], BF16)
        ones = self.sb("ones", [128, 128], BF16)
        onesD = self.sb("onesD", [128, 128], BF16)
        hvt = self.sb("hvt", [128, 64], BF16)
        hvf = self.sb("hvf", [128, 1], F32)
        gains = self.sb("gains", [128, 192], F32)
        esb = self.sb("esb", [128, 16], F32)
        scp = self.sb("scp", [128, 2, KC, 16], F32)
        uprev = self.sb("uprev", [128, 2, KC, 2], F32)
        self.ps = [self.st.enter_context(nc.psum_tensor("ps%d" % i, [128, 512], F32)) for i in range(8)]
        self.psr = self.ring("ps", 6)
        ps = self.ps
        ps6, ps7 = ps[6], ps[7]
        bps6, bps7 = self.bb("ps", 6), self.bb("ps", 7)

        misc = [S.new_chan("mc%d" % i) for i in range(6)]
        mr = self.ring("misc", 6)

        def mchan():
            return misc[self.nxt(mr)]

        pmisc = [S.new_chan("pmc%d" % i) for i in range(2)]
        pmr = self.ring("pmisc", 2)

        def pchan():
            return pmisc[self.nxt(pmr)]

        bb = self.bb
        bl = self.bl
        b_ident, b_identb, b_ones, b_onesD = bb("ident"), bb("identb"), bb("ones"), bb("onesD")
        b_gains, b_esb, b_E, b_En, b_hv = bb("gains"), bb("esb"), bb("E"), bb("En"), bb("hv")
        allbig = lambda chs: [bb("big", ch, b) for ch in chs for b in range(NBLK)]

        tmpc = big32[:, 7320:7320 + NT]
        acc = big32[:, 8540:8540 + NT]
        up = big32[:, 9760:9760 + 2 + CS0]
        us = big32[:, 10920:11000].rearrange("p (s t) -> p s t", s=NSEQ)
        cs = big32[:, 11000:11144].rearrange("p (c t) -> p c t", c=KC)
        b_tmpc, b_acc, b_up, b_us, b_cs = bb("tmpc"), bb("acc"), bb("up"), bb("us"), bb("cs")

        self.stage("pro0")
        self.dma("sp", ident[:], ident_d, [], [b_ident], mchan())
        self.cp(identb[:], ident[:], [b_ident], [b_identb])
        S.op("dve", lambda e: e.memset(ones[:], 1.0), [], [b_ones])
        S.op("dve", lambda e: e.memset(onesD[:], 1.0 / D), [], [b_onesD])
        vrows = vecs.rearrange("v (c p) -> (v c) p", p=128)
        for half in range(2):
            sg, bsg, chn = self.stg_next()
            self.dma("sp", sg[0:96, 0:128], vrows[half * 96:(half + 1) * 96, :], [], [bsg], chn)
            pt, bpt = self.ps_next()
            self.pe_tr([(pt[:, 0:96], sg[0:96, 0:128], ident[0:96, 0:96])], [bsg, b_ident], [bpt])
            self.cp(gains[:, half * 96:(half + 1) * 96], pt[:, 0:96], [bpt], [b_gains])
        G = lambda v, c: gains[:, v * 8 + c: v * 8 + c + 1]
        for par in range(2):
            self.dma("sp", esb[par * 64:(par + 1) * 64, :], sinks[par:par + 1, :].partition_broadcast(64), [], [b_esb], mchan())
        self.act(esb[:], esb[:], AF.Exp, [b_esb], [b_esb])

        self.stage("pro1")
        Jt = big32[:, 0:383]
        OHt = big32[0:32, 400:528]
        BDt = big32[0:64, 600:664]
        ttb = big32[:, 700:716]
        rbt = big32[0:32, 720:736]
        b_pro = bb("pro")
        Jb = bigT[:, 1600:1983]
        ttb16 = bigT[:, 2000:2016]
        self.dma("pool", Jb, J_d, [], [b_pro], pchan())
        self.dma("sp", OHt, OH_d, [], [b_pro], mchan())
        self.dma("sp", BDt, BD_d, [], [b_pro], mchan())
        self.dma("sp", rbt, relb, [], [b_pro], mchan())
        pt, bpt = self.ps_next()
        self.pe([(pt[:, 0:16], OHt, rbt, True, True)], [b_pro], [bpt])
        b_tt = bb("ttb")
        self.act(ttb16, pt[:, 0:16], AF.Exp, [bpt, b_pro], [b_tt])
        def e_group(j, qb):
            pt, bpt = self.ps_next()
            lst = []
            for qi in range(32):
                q = qb * 32 + qi
                off = (127 - q) if j == 0 else (255 - q)
                lst.append((pt[:, qi * 16:(qi + 1) * 16], Jb[:, off:off + 128], ttb16, True, True))
            self.pe(lst, [b_pro, b_tt], [bpt])
            self.cp(E[:, j, :, qb * 32:(qb + 1) * 32], pt[:, 0:512].rearrange("p (q h) -> p h q", h=16), [bpt], [b_E])
        e_groups = [(j, qb) for j in range(2) for qb in range(4)]
        self.stage("pro3")
        xm = x[:, :, 0:256]
        hm = h[:, :, 0:256]
        b_xm, b_hm = bb("xm"), bb("hm")
        for rb_ in range(2):
            sg, bsg, chn = self.stg_next()
            self.dma("sp", sg[:], mem[rb_ * 128:(rb_ + 1) * 128, :], [], [bsg], chn)
            for half in range(2):
                pt, bpt = self.ps_next()
                self.pe_tr([(pt[:, i * 128:(i + 1) * 128], sg[:, (half * 4 + i) * 128:(half * 4 + i + 1) * 128], ident[:]) for i in range(4)],
                           [bsg, b_ident], [bpt])
                self.cp(x[:, half * 4:half * 4 + 4, rb_ * 128:(rb_ + 1) * 128], pt[:, 0:512].rearrange("p (c t) -> p c t", c=4), [bpt], [b_xm], eng="act")
        self.stage("pro4")
        for m in range(KC):
            self.act(hm[:, m, :], xm[:, m, :], AF.Square, [b_xm], [b_hm])
        pt, bpt = self.ps_next()
        self.pe([(pt[:, 0:256], onesD[:], hm[:, m, :], m == 0, m == KC - 1) for m in range(KC)], [b_hm, b_onesD], [bpt])
        rm, brm = self.t32_next()
        self.act(rm[:, 0:256], pt[:, 0:256], AF.Ln, [bpt], [brm], bias=RMS_EPS)
        self.act(rm[:, 0:256], rm[:, 0:256], AF.Exp, [brm], [brm], scale=-0.5)
        self.stage("pro5")
        for l in range(DEPTH):
            b_msc = bb("mscr", l)
            if l == 1:
                self.stage("pro8")
            hm = h[:, :, (l % 2) * 256:(l % 2 + 1) * 256]
            b_hm = bb("hm", l % 2)
            for m in range(KC):
                self.stt(hm[:, m, :], xm[:, m, :], G(12 + l, m), rm[:, 0:256], ALU.mult, ALU.mult, [b_xm, brm, b_gains], [b_hm])
            Wm = w_mkv[l].rearrange("(k p) n -> p k n", p=128)
            if l == 0:
                self.stage("pro6")
            tk, btk = self.t16_next()
            tk2, btk2 = self.t16_next()
            for hh in range(4):
                wv_, bw = self.wload(Wm[:, :, hh * 128:(hh + 1) * 128], KC, 128)
                pt, bpt = self.ps_next()
                self.pe([(pt[:, 0:256], wv_[:, k, :], hm[:, k, :], k == 0, k == KC - 1) for k in range(KC)], [bw, b_hm], [bpt])
                dst = (tk if hh < 2 else tk2)[:, (hh % 2) * 256:(hh % 2 + 1) * 256]
                self.cp(dst, pt[:, 0:256], [bpt], [btk if hh < 2 else btk2], eng="act")
            self.dma("sp", mkt_s[l][:, 0:512], tk[:], [btk], [b_msc], mchan())
            self.dma("sp", mkt_s[l][:, 512:1024], tk2[:], [btk2], [b_msc], mchan())
            if l == 0:
                self.stage("pro7")
            sgs = [self.stg_next() for _ in range(2)]
            tv, btv = self.t16_next()
            tv2, btv2 = self.t16_next()
            for cg in range(4):
                wv_, bw = self.wload(Wm[:, :, cg * 256:(cg + 1) * 256], KC, 256)
                for mc in range(2):
                    pt, bpt = self.ps_next()
                    self.pe([(pt[:, 0:256], hm[:, k, mc * 128:(mc + 1) * 128], wv_[:, k, :], k == 0, k == KC - 1) for k in range(KC)], [bw, b_hm], [bpt])
                    sg, bsg, chn = sgs[mc]
                    self.cp(sg[:, cg * 256:(cg + 1) * 256], pt[:, 0:256], [bpt], [bsg], eng="act")
                    if cg >= 2:
                        tvv, btvv = (tv, btv) if mc == 0 else (tv2, btv2)
                        self.cp(tvv[:, (cg - 2) * 256:(cg - 1) * 256], pt[:, 0:256], [bpt], [btvv])
            for mc in range(2):
                sg, bsg, chn = sgs[mc]
                self.dma("sp", mko[l][mc * 128:(mc + 1) * 128, :], sg[:, 0:512], [bsg], [], chn)
                self.dma("sp", mvo[l][mc * 128:(mc + 1) * 128, :], sg[:, 512:1024], [bsg], [], chn)
            self.dma("sp", mv_s[l][0], tv[:], [btv], [b_msc], mchan())
            self.dma("sp", mv_s[l][1], tv2[:], [btv2], [b_msc], mchan())
            for _ in range(2):
                e_group(*e_groups.pop(0))
        self.stage("pro2")
        Ev = E[:].rearrange("p j (a par) q -> p j a par q", par=2)
        for par in range(2):
            self.tt(En[:, par, :].rearrange("p (a q) -> p a q", a=8), Ev[0:64, 1, :, par, 0:64],
                    BDt.unsqueeze(1).to_broadcast([64, 8, 64]), ALU.mult, [b_E, b_pro], [b_En])

        S.op("dve", lambda e: e.memset(hvt[0:1, 0:1], 0.0), [b_pro, b_tt, b_xm, bb("hm"), bb("hm", 0), bb("hm", 1), brm],
             allbig(range(FC)) + bl("x", range(KC), 0, NT) + bl("h", range(KC), 0, NT) + [b_hv])

        self.stage("pro")
        def norm_sq(t0, t1):
            for m in range(KC):
                self.act(h[:, m, t0:t1], x[:, m, t0:t1], AF.Square, bl("x", [m], t0, t1), bl("h", [m], t0, t1))

        def norm_rest(gv, t0, t1, final=False):
            n = t1 - t0
            pt, bpt = self.ps_next()
            self.pe([(pt[:, 0:n], onesD[:], h[:, m, t0:t1], m == 0, m == KC - 1) for m in range(KC)],
                    bl("h", range(KC), t0, t1) + [b_onesD], [bpt])
            rs, brs = self.t32_next()
            self.act(rs[:, 0:n], pt[:, 0:n], AF.Ln, [bpt], [brs], bias=RMS_EPS)
            self.act(rs[:, 0:n], rs[:, 0:n], AF.Exp, [brs], [brs], scale=-0.5)
            for m in range(KC):
                if final:
                    self.stt(x[:, m, t0:t1], x[:, m, t0:t1], G(gv, m), rs[:, 0:n], ALU.mult, ALU.mult,
                             bl("x", [m], t0, t1) + [brs, b_gains], bl("x", [m], t0, t1))
                else:
                    self.stt(h[:, m, t0:t1], x[:, m, t0:t1], G(gv, m), rs[:, 0:n], ALU.mult, ALU.mult,
                             bl("x", [m], t0, t1) + [brs, b_gains], bl("h", [m], t0, t1))

        def rmsnorm(gv, c0, c1, final=False):
            for (t0, t1) in tiles(c0, c1):
                norm_sq(t0, t1)
                norm_rest(gv, t0, t1, final)

        def out_proj(W3, nk, src_chunks, c0, c1, scale, nxt, halo_mask=False):
            tl = tiles(c0, c1)

            def group(m, wv_, bw, t0, t1):
                n = t1 - t0
                py, bpy = self.ps_next()
                self.pe([(py[:, 0:n], wv_[:, j, :], big[:, src_chunks[j], t0:t1], j == 0, j == nk - 1) for j in range(nk)],
                        bl("big", src_chunks, t0, t1) + [bw], [bpy])
                xb = bl("x", [m], t0, t1)
                if scale == 1.0:
                    self.tt(x[:, m, t0:t1], py[:, 0:n], x[:, m, t0:t1], ALU.add, [bpy] + xb, xb)
                else:
                    self.stt(x[:, m, t0:t1], py[:, 0:n], scale, x[:, m, t0:t1], ALU.mult, ALU.add, [bpy] + xb, xb)
                if halo_mask and t0 < HALO:
                    e1 = min(t1, HALO)
                    self.ts(x[:, m, t0:e1], x[:, m, t0:e1], hvf[:, 0:1], ALU.mult, bl("x", [m], t0, e1) + [bb("hvf")], bl("x", [m], t0, e1))
                if nxt is not None:
                    s0 = max(t0, nxt[1])
                    if s0 < t1:
                        self.act(h[:, m, s0:t1], x[:, m, s0:t1], AF.Square, bl("x", [m], s0, t1), bl("h", [m], s0, t1))

            for m in range(KC - 2):
                wv_, bw = self.wload(W3[:, :, m * 128:(m + 1) * 128], nk, 128)
                for (t0, t1) in tl:
                    group(m, wv_, bw, t0, t1)
            w6, b6 = self.wload(W3[:, :, 6 * 128:7 * 128], nk, 128)
            w7, b7 = self.wload(W3[:, :, 7 * 128:8 * 128], nk, 128)
            ntl = tiles(nxt[1], nxt[2]) if nxt is not None else []
            rest_done = 0
            prev_end = None
            for (t0, t1) in tl:
                group(6, w6, b6, t0, t1)
                group(7, w7, b7, t0, t1)
                if nxt is not None and prev_end is not None:
                    while rest_done < len(ntl) and ntl[rest_done][1] <= prev_end:
                        norm_rest(nxt[0], ntl[rest_done][0], ntl[rest_done][1], nxt[3])
                        rest_done += 1
                prev_end = t1
            deferred = None
            if nxt is not None:
                while rest_done < len(ntl):
                    norm_rest(nxt[0], ntl[rest_done][0], ntl[rest_done][1], nxt[3])
                    rest_done += 1
            return deferred

        def ffn(Wg, Wu, Wd, gv, c0, c1, skip_norm=False, nxt=None, pre=None):
            if not skip_norm:
                rmsnorm(gv, c0, c1)
            npre = 0
            tl = tiles(c0, c1)
            Wg3 = Wg.rearrange("(k p) n -> p k n", p=128)
            Wu3 = Wu.rearrange("(k p) n -> p k n", p=128)
            Wd3 = Wd.rearrange("(j p) n -> p j n", p=128)
            for jp in range(FC // 2):
                wg, bwg = self.wload(Wg3[:, :, jp * 256:(jp + 1) * 256], KC, 256)
                wu, bwu = self.wload(Wu3[:, :, jp * 256:(jp + 1) * 256], KC, 256)
                if jp == 0 and len(tl) > 1:
                    order = [(jj, t) for t in tl[:-1] for jj in range(2)] + [(jj, tl[-1]) for jj in range(2)]
                else:
                    order = [(jj, t) for jj in range(2) for t in tl]
                for gi, (jj, (t0, t1)) in enumerate(order):
                    j = jp * 2 + jj
                    if pre is not None and jp == 0 and (gi == 2 * (len(tl) - 1) or len(tl) == 1):
                        pre()
                        pre = None
                    if True:
                        n = t1 - t0
                        hb = bl("h", range(KC), t0, t1)
                        pg, bpg = self.ps_next()
                        self.pe([(pg[:, 0:n], wg[:, k, jj * 128:(jj + 1) * 128], h[:, k, t0:t1], k == 0, k == KC - 1) for k in range(KC)], hb + [bwg], [bpg])
                        pu, bpu = self.ps_next()
                        self.pe([(pu[:, 0:n], wu[:, k, jj * 128:(jj + 1) * 128], h[:, k, t0:t1], k == 0, k == KC - 1) for k in range(KC)], hb + [bwu], [bpu])
                        sg, bsg = self.t32_next()
                        self.act(sg[:, 0:n], pg[:, 0:n], AF.Silu, [bpg], [bsg])
                        self.tt(big[:, j, t0:t1], sg[:, 0:n], pu[:, 0:n], ALU.mult, [bsg, bpu], bl("big", [j], t0, t1))
            return out_proj(Wd3, FC, list(range(FC)), c0, c1, 0.5, nxt)

        def wout(Wo, c0, c1, nxt=None, halo_mask=False):
            return out_proj(Wo.rearrange("(k p) n -> p k n", p=128), 12, list(range(12)), c0, c1, 1.0, nxt, halo_mask)

        def proj_group(wv_, bw, dst_chunk, t0, t1):
            n = t1 - t0
            pt, bpt = self.ps_next()
            self.pe([(pt[:, 0:n], wv_[:, k, :], h[:, k, t0:t1], k == 0, k == KC - 1) for k in range(KC)],
                    bl("h", range(KC), t0, t1) + [bw], [bpt])
            self.cp(big[:, dst_chunk, t0:t1], pt[:, 0:n], [bpt], bl("big", [dst_chunk], t0, t1), eng="act")

        def proj_to_big(W3, specs, c0, c1, pre=None):
            tl = tiles(c0, c1)
            specs = list(specs)
            if len(specs) >= 2 and len(tl) > 1:
                ws = [self.wload(W3[:, :, wc:wc + 128], KC, 128) for (wc, _) in specs[:2]]
                for t in tl[:-1]:
                    for i in range(2):
                        proj_group(ws[i][0], ws[i][1], specs[i][1], t[0], t[1])
                if pre is not None:
                    pre()
                    pre = None
                for i in range(2):
                    proj_group(ws[i][0], ws[i][1], specs[i][1], tl[-1][0], tl[-1][1])
                specs = specs[2:]
            if pre is not None:
                pre()
            for (wc, dst) in specs:
                wv_, bw = self.wload(W3[:, :, wc:wc + 128], KC, 128)
                for (t0, t1) in tl:
                    proj_group(wv_, bw, dst, t0, t1)

        self.kmch = [S.new_chan("kmch%d" % i) for i in range(2)]
        self.vmch = [S.new_chan("vmch%d" % i) for i in range(2)]
        kcch = [S.new_chan("kcch%d" % i) for i in range(2)]
        vcch = [S.new_chan("vcch%d" % i) for i in range(2)]
        b_KT = lambda c0, c1: bl("KT", [0], c0, c1)
        b_V = lambda blk: bb("V", blk)
        Ev6 = E[:].rearrange("p j (hk ge par) q -> p j hk ge par q", hk=4, ge=2)
        t16q = [self.sb("t16q%d" % i, [128, 128], BF16) for i in range(4)]
        t16qr = self.ring("t16q", 4)

        def t16q_next():
            i = self.nxt(t16qr)
            return t16q[i], bb("t16q", i)

        def pipeline(items):
            nst = max(len(it) for it in items)
            for step in range(len(items) + nst - 1):
                for s_ in range(nst):
                    i = step - s_
                    if 0 <= i < len(items) and s_ < len(items[i]) and items[i][s_] is not None:
                        items[i][s_]()

        def interleave(a, b):
            out = []
            nb_ = len(b)
            if nb_ == 0:
                return list(a)
            per = max(1, len(a) // nb_)
            bi = 0
            for i, it in enumerate(a):
                out.append(it)
                if (i + 1) % per == 0 and bi < nb_:
                    out.append(b[bi])
                    bi += 1
            out.extend(b[bi:])
            return out

        def mem_prefetch(l, p, s):
            i2 = s % 2
            self.dma("pool", kmc[i2][:], cmk[l, p, s].rearrange("(c m) n -> m c n", m=128), [], [bb("kmc", i2)], self.kmch[i2])
            self.dma("pool", vmc[i2][:], cmv[l, p, s].rearrange("(c m) n -> m c n", m=128), [], [bb("vmc", i2)], self.vmch[i2])

        def swa_prefetch(p, s):
            i2 = s % 2
            src = ck[p, s].rearrange("k (a d) -> k a d", a=4)
            for dup in range(2):
                self.dma("pool", kc16[i2][:, :, dup, :], src, [], [bb("kc16", i2)], kcch[i2], join=(dup == 1))
            self.dma("pool", vc16[i2][:], cv[p, s], [], [bb("vc16", i2)], vcch[i2])

        def mem_attn(l, p, c0):
            sc = 128 ** -0.5
            b_mkt, b_mv = bb("mKT"), bb("mV")
            b_msc = bb("mscr", l)
            self.dma("sp", mKT[:].rearrange("p a b -> p (a b)"), mkt_s[l], [b_msc], [b_mkt], mchan())
            self.dma("sp", mV[:], mv_s[l].rearrange("c p n -> p c n"), [b_msc], [b_mv], mchan())
            qs = bl("big", range(8, 12), CS0, NT)

            def prompt_item(t0, t1, hh):
                n = t1 - t0
                qb = bl("big", [8 + hh], t0, t1)
                ctx = {}

                def s0():
                    pts = []
                    for mc in range(2):
                        s_, bs_ = self.ps_next()
                        self.pe([(s_[:, 0:n], mKT[:, hh, mc * 128:(mc + 1) * 128], big[:, 8 + hh, t0:t1], True, True)], qb + [b_mkt], [bs_])
                        pp, bpp = self.t16_next()
                        self.act(pp[:, 0:n], s_[:, 0:n], AF.Exp, [bs_], [bpp], scale=sc)
                        pts.append((pp, bpp))
                    ctx["pts"] = pts

                def s1():
                    pts = ctx["pts"]
                    pa, bpa = self.ps_next()
                    self.pe([(pa[:, 0:n], mV[:, mc, hh * 128:(hh + 1) * 128], pts[mc][0][:, 0:n], mc == 0, mc == 1) for mc in range(2)],
                            [b_mv, pts[0][1], pts[1][1]], [bpa])
                    pb, bpb = self.ps_next()
                    self.pe([(pb[:, 0:n], ones[:], pts[mc][0][:, 0:n], mc == 0, mc == 1) for mc in range(2)],
                            [b_ones, pts[0][1], pts[1][1]], [bpb])
                    rb_, brb = self.t32_next()
                    self.act(rb_[:, 0:n], pb[:, 0:n], AF.Ln, [bpb], [brb])
                    self.act(rb_[:, 0:n], rb_[:, 0:n], AF.Exp, [brb], [brb], scale=-1.0)
                    self.tt(big[:, 8 + hh, t0:t1], pa[:, 0:n], rb_[:, 0:n], ALU.mult, [bpa, brb], qb)
                return [s0, s1]

            def seq_item(s):
                i2 = s % 2
                b_km, b_vm, b_kT = bb("kmc", i2), bb("vmc", i2), bb("kmT")
                cq0 = CS0 + s * 8
                ctx = {}

                def r1():
                    pt, bpt = self.ps_next()
                    ptb = pt.bitcast(BF16)
                    self.pe_tr([(ptb[:, (hh * 2 + mc) * 128:(hh * 2 + mc + 1) * 128], kmc[i2][:, mc, hh * 128:(hh + 1) * 128], identb[:])
                                for hh in range(4) for mc in range(2)], [b_km, b_identb], [bpt])
                    self.cp(kmT[:].rearrange("p a b c -> p (a b c)"), ptb[:, 0:1024], [bpt], [b_kT])

                def r2():
                    s_, bs_ = self.ps_next()
                    self.pe([(s_[:, (hh * 2 + mc) * 8:(hh * 2 + mc + 1) * 8], kmT[:, hh, mc, :], big[:, 8 + hh, cq0:cq0 + 8], True, True)
                             for hh in range(4) for mc in range(2)], qs + [b_kT], [bs_])
                    pp, bpp = t16q_next()
                    self.act(pp[:, 0:64], s_[:, 0:64], AF.Exp, [bs_], [bpp], scale=sc)
                    ctx["pp"] = (pp, bpp)

                def r3():
                    pp, bpp = ctx["pp"]
                    first = (s == 0)
                    lst = []
                    for hh in range(4):
                        for mc in range(2):
                            lst.append((ps6[:, s * 32 + hh * 8: s * 32 + hh * 8 + 8], vmc[i2][:, mc, hh * 128:(hh + 1) * 128],
                                        pp[:, (hh * 2 + mc) * 8:(hh * 2 + mc + 1) * 8], first and hh == 0 and mc == 0,
                                        s == NSEQ - 1 and hh == 3 and mc == 1))
                    self.pe(lst, [b_vm, bpp], [bps6])
                    ppv = pp[:, 0:64].rearrange("p (h c i) -> p h c i", h=4, c=2)
                    self.pe([(ps7[:, s * 32:(s + 1) * 32].rearrange("p (h i) -> p h i", h=4), ones[:], ppv[:, :, mc, :], first and mc == 0, s == NSEQ - 1 and mc == 1)
                             for mc in range(2)], [b_ones, bpp], [bps7])
                    if s + 2 < NSEQ:
                        mem_prefetch(l, p, s + 2)
                return [r1, r2, r3]

            def fin():
                rb_, brb = self.t32_next()
                self.act(rb_[:, 0:256], ps7[:, 0:256], AF.Ln, [bps7], [brb])
                self.act(rb_[:, 0:256], rb_[:, 0:256], AF.Exp, [brb], [brb], scale=-1.0)
                self.tt(big[:, 8:12, CS0:NT].rearrange("p h (s i) -> p h s i", s=NSEQ),
                        ps6[:, 0:256].rearrange("p (s h i) -> p h s i", s=NSEQ, h=4),
                        rb_[:, 0:256].rearrange("p (s h i) -> p h s i", s=NSEQ, h=4), ALU.mult, [bps6, brb], qs)

            pit = [prompt_item(t0, t1, hh) for (t0, t1) in tiles(c0, CS0) for hh in range(4)]
            sit = [seq_item(s) for s in range(NSEQ)]
            pipeline(interleave(pit, sit) + [[None, None, fin]])

        def swa_attn(jl, p):
            sc = 64 ** -0.5
            qs = bl("big", range(8), CS0, NT)
            ktn = b_KT(CS0, NT)

            def prompt_item(nb, hk):
                cq0 = CM0 + 128 * nb
                cq1 = cq0 + 128
                kcols = [(4, 132) if nb == 0 else (cq0 - 128, cq0), (cq0, cq1)]
                vblk = [nb, nb + 1]
                qb = bl("big", [2 * hk, 2 * hk + 1], cq0, cq1)
                ctx = {}

                def s0():
                    PT = []
                    for par in range(2):
                        s_, bs_ = self.ps_next()
                        pr = slice(par * 64, (par + 1) * 64)
                        self.pe([(s_[:, j * 256:(j + 1) * 256].rearrange("p (g q) -> p g q", g=2),
                                  KT[pr, hk, kcols[j][0] - 4:kcols[j][1] - 4], big[pr, 2 * hk:2 * hk + 2, cq0:cq1], True, True) for j in range(2)],
                                qb + b_KT(kcols[0][0], kcols[0][1]) + b_KT(cq0, cq1), [bs_])
                        pt_, bpt_ = self.t16_next()
                        self.act(pt_[:], s_[:, 0:512], AF.Exp, [bs_], [bpt_], scale=sc)
                        self.tt(pt_[:].rearrange("p (j g q) -> p j g q", j=2, g=2), pt_[:].rearrange("p (j g q) -> p j g q", j=2, g=2),
                                Ev6[:, :, hk, :, par, :], ALU.mult, [bpt_, b_E], [bpt_])
                        PT.append((pt_, bpt_))
                    ctx["PT"] = PT

                def s1():
                    PT = ctx["PT"]
                    pa, bpa = self.ps_next()
                    pb, bpb = self.ps_next()
                    la, lb_ = [], []
                    for par in range(2):
                        pr = slice(par * 64, (par + 1) * 64)
                        for j in range(2):
                            la.append((pa[pr, 0:256], V[:, vblk[j], hk * 64:(hk + 1) * 64], PT[par][0][:, j * 256:(j + 1) * 256], j == 0, j == 1))
                            on = hvt[:, 0:64] if (nb == 0 and j == 0) else ones[:, 0:64]
                            lb_.append((pb[pr, 0:256], on, PT[par][0][:, j * 256:(j + 1) * 256], j == 0, j == 1))
                    self.pe(la, [b_V(vblk[0]), b_V(vblk[1]), PT[0][1], PT[1][1]], [bpa])
                    self.pe(lb_, [b_ones, b_hv, PT[0][1], PT[1][1]], [bpb])
                    rb_, brb = self.t32_next()
                    for ge in range(2):
                        col = jl * 8 + hk * 2 + ge
                        self.act(rb_[:, ge * 128:(ge + 1) * 128], pb[:, ge * 128:(ge + 1) * 128], AF.Ln, [bpb, b_esb], [brb], bias=esb[:, col:col + 1])
                    self.act(rb_[:, 0:256], rb_[:, 0:256], AF.Exp, [brb], [brb], scale=-1.0)
                    self.tt(big[:, 2 * hk:2 * hk + 2, cq0:cq1], pa[:, 0:256].rearrange("p (g q) -> p g q", g=2),
                            rb_[:, 0:256].rearrange("p (g q) -> p g q", g=2), ALU.mult, [bpa, brb], qb)
                return [s0, s1]

            def new_item():
                ctx = {}

                def n0():
                    PTn = []
                    for par in range(2):
                        pr = slice(par * 64, (par + 1) * 64)
                        s_, bs_ = self.ps_next()
                        self.pe([(s_[0:64, hk * 128:(hk + 1) * 128].rearrange("p (g q) -> p g q", g=2), KT[pr, hk, CS0 - 4:NT - 4],
                                  big[pr, 2 * hk:2 * hk + 2, CS0:NT], True, True) for hk in range(4)], qs + ktn, [bs_])
                        pt_, bpt_ = self.t16_next()
                        self.act(pt_[0:64, :], s_[0:64, 0:512], AF.Exp, [bs_], [bpt_], scale=sc)
                        self.tt(pt_[0:64, :], pt_[0:64, :], En[:, par, :], ALU.mult, [bpt_, b_En], [bpt_])
                        PTn.append((pt_, bpt_))
                    ctx["PTn"] = PTn

                def n1():
                    PTn = ctx["PTn"]
                    la, lb_ = [], []
                    for par in range(2):
                        pr = slice(par * 64, (par + 1) * 64)
                        for hk in range(4):
                            la.append((ps6[pr, hk * 128:(hk + 1) * 128], V[0:64, 9, hk * 64:(hk + 1) * 64], PTn[par][0][0:64, hk * 128:(hk + 1) * 128], hk == 0, False))
                        lb_.append((ps7[pr, 0:512], ones[0:64, 0:64], PTn[par][0][0:64, 0:512], True, False))
                    self.pe(la, [b_V(9), PTn[0][1], PTn[1][1]], [bps6])
                    self.pe(lb_, [b_ones, PTn[0][1], PTn[1][1]], [bps7])
                return [n0, n1]

            def seq_item(s):
                i2 = s % 2
                b_kc, b_kT, b_vc = bb("kc16", i2), bb("kcT", i2), bb("vc16", i2)
                cq0 = CS0 + s * 8
                ctx = {}

                def q1():
                    pt, bpt = self.ps_next()
                    ptb = pt.bitcast(BF16)
                    self.pe_tr([(ptb[:, hk * 128:(hk + 1) * 128], kc16[i2][:, hk, :, :].rearrange("p a d -> p (a d)"), identb[:]) for hk in range(4)],
                               [b_kc, b_identb], [bpt])
                    self.cp(kcT[i2][:].rearrange("p a k -> p (a k)"), ptb[:, 0:512], [bpt], [b_kT], eng="act")

                def q2():
                    PTc = []
                    for par in range(2):
                        pr = slice(par * 64, (par + 1) * 64)
                        s_, bs_ = self.ps_next()
                        self.pe([(s_[:, hk * 32:(hk + 1) * 32].rearrange("p (g c) -> p g c", g=2)[:, :, 0:8], kcT[i2][pr, hk, :],
                                  big[pr, 2 * hk:2 * hk + 2, cq0:cq0 + 8], True, True)
                                 for hk in range(4)], qs + [b_kT], [bs_])
                        pt_, bpt_ = t16q_next()
                        pv8 = pt_[:, 0:128].rearrange("p (a c) -> p a c", a=8)[:, :, 0:8]
                        self.act(pv8, s_[:, 0:128].rearrange("p (a c) -> p a c", a=8)[:, :, 0:8], AF.Exp, [bs_], [bpt_], scale=sc)
                        self.tt(pv8, pv8, Ev[:, 0, :, par, 0:8], ALU.mult, [bpt_, b_E], [bpt_])
                        PTc.append((pt_, bpt_))
                    ctx["PTc"] = PTc

                def q3():
                    PTc = ctx["PTc"]
                    la, lb_ = [], []
                    for par in range(2):
                        pr = slice(par * 64, (par + 1) * 64)
                        for hk in range(4):
                            la.append((ps6[pr, hk * 128:(hk + 1) * 128].rearrange("p (g s i) -> p g s i", g=2, s=NSEQ)[:, :, s, :],
                                       vc16[i2][:, hk * 64:(hk + 1) * 64], PTc[par][0][:, hk * 32:(hk + 1) * 32].rearrange("p (g c) -> p g c", g=2)[:, :, 0:8],
                                       False, s == NSEQ - 1 and hk == 3))
                        lb_.append((ps7[pr, 0:512].rearrange("p (a s i) -> p a s i", a=8, s=NSEQ)[:, :, s, :], ones[:, 0:64],
                                    PTc[par][0][:, 0:128].rearrange("p (a c) -> p a c", a=8)[:, :, 0:8], False, s == NSEQ - 1))
                    self.pe(la, [b_vc, PTc[0][1], PTc[1][1]], [bps6])
                    self.pe(lb_, [b_ones, PTc[0][1], PTc[1][1]], [bps7])
                    if s + 2 < NSEQ:
                        swa_prefetch(p, s + 2)
                return [q1, q2, q3]

            def fin():
                rb_, brb = self.t32_next()
                for a in range(8):
                    col = jl * 8 + a
                    self.act(rb_[:, a * 64:(a + 1) * 64], ps7[:, a * 64:(a + 1) * 64], AF.Ln, [bps7, b_esb], [brb], bias=esb[:, col:col + 1])
                self.act(rb_[:], rb_[:], AF.Exp, [brb], [brb], scale=-1.0)
                self.tt(big[:, 0:8, CS0:NT], ps6[:, 0:512].rearrange("p (a q) -> p a q", a=8), rb_[:].rearrange("p (a q) -> p a q", a=8),
                        ALU.mult, [bps6, brb], qs)

            pit = [prompt_item(nb, hk) for nb in range(8) for hk in range(4)]
            sit = [seq_item(s) for s in range(NSEQ)]
            pipeline([new_item()] + interleave(pit, sit) + [[None, None, fin]])

        for p in range(NPASS):
            self.dma("pool", hvt[:], hv_d[p], [], [b_hv], pchan())
            self.dma("sp", hvf[:], hv_d[p][:, 0:1], [], [bb("hvf")], mchan(), allow_slow_non_contiguous=True)
            b_scp = bb("scp")
            for l in range(2):
                sg, bsg, chn = self.stg_next()
                self.dma("sp", sg[0:16, :], scv[p, l], [], [bsg], chn)
                for half in range(2):
                    pt, bpt = self.ps_next()
                    self.pe_tr([(pt[:, i * 16:(i + 1) * 16], sg[0:16, (half * 4 + i) * 128:(half * 4 + i + 1) * 128], ident[0:16, 0:16]) for i in range(4)],
                               [bsg, b_ident], [bpt])
                    self.cp(scp[:, l, half * 4:half * 4 + 4, :], pt[:, 0:64].rearrange("p (c t) -> p c t", c=4), [bpt], [b_scp])
            self.stage("p%da" % p)
            self.stage("p%db" % p)
            if p == 0:
                xblocks = [(i * 128, (i + 1) * 128) for i in range(9)] + [(1152, 1216), (1216, NT)]
            else:
                xblocks = [(CM0 + i * 128, CM0 + (i + 1) * 128) for i in range(8)] + [(CS0, NT)]
            n0_tiles = tiles(0 if p == 0 else CM0, NT)
            n0_done = 0
            for (r0, r1) in xblocks:
                n = r1 - r0
                sg, bsg, chn = self.stg_next()
                self.dma("sp", sg[0:n, :], xin[p, r0:r1, :], [], [bsg], chn)
                for half in range(2):
                    pt, bpt = self.ps_next()
                    self.pe_tr([(pt[:, i * 128:i * 128 + n], sg[0:n, (half * 4 + i) * 128:(half * 4 + i + 1) * 128], ident[0:n, 0:n]) for i in range(4)],
                               [bsg, b_ident], [bpt])
                    self.cp(x[:, half * 4:half * 4 + 4, r0:r1], pt[:, 0:512].rearrange("p (c t) -> p c t", c=4)[:, :, 0:n],
                            [bpt], bl("x", range(half * 4, half * 4 + 4), r0, r1), eng=("act" if half == 0 else "dve"))
                while n0_done < len(n0_tiles) and n0_tiles[n0_done][1] <= r1:
                    norm_sq(n0_tiles[n0_done][0], n0_tiles[n0_done][1])
                    norm_rest(0, n0_tiles[n0_done][0], n0_tiles[n0_done][1])
                    n0_done += 1

            self.dma("sp", sks[p][:, 0:120, :], ck[p][:, 8:128, :], [], [], mchan())
            self.dma("sp", svs[p][:, 0:120, :], cv[p][:, 8:128, :], [], [], mchan())
            self.stage("p%dload" % p)
            dfr = None
            for l in range(DEPTH):
                c0 = 0 if (l < 2 and p == 0) else CM0
                if l == 2:
                    if dfr is not None:
                        dfr()
                        dfr = None
                    Wk3 = w_kv.rearrange("(k p) n -> p k n", p=128)
                    for hp in range(2):
                        i = self.nxt(self.wr)
                        wkd = self.wt[i][:, 0:KC * 256].rearrange("p (k a u d) -> p k a u d", k=KC, a=2, u=2)
                        bw = bb("w", i)
                        for dup in range(2):
                            for a_ in range(2):
                                self.dma("pool", wkd[:, :, a_, dup, :], Wk3[:, :, hp * 128 + a_ * 64:hp * 128 + (a_ + 1) * 64],
                                         [], [bw], self.wc[i], join=(dup + a_ > 0))
                        for a_ in range(2):
                            hk = hp * 2 + a_
                            if p == 1 and a_ == 0 and hp == 0:
                                self.cp(KT[:, :, 0:128], KT[:, :, 1024:1152], b_KT(1028, 1156), b_KT(4, 132))
                                self.cp(V[:, 0, :], V[:, 8, :], [b_V(8)], [b_V(0)])
                            for (t0, t1) in tiles(4 if p == 0 else CM0, NT):
                                n = t1 - t0
                                pt, bpt = self.ps_next()
                                self.pe([(pt[:, 0:n], wkd[:, k, a_, :, :].rearrange("p u d -> p (u d)"), h[:, k, t0:t1], k == 0, k == KC - 1) for k in range(KC)],
                                        bl("h", range(KC), t0, t1) + [bw], [bpt])
                                self.cp(KT[:, hk, t0 - 4:t1 - 4], pt[:, 0:n], [bpt], b_KT(t0, t1), eng="act")
                    wkt, bwk = self.wload(Wk3[:, :, 0:256], KC, 256)
                    wvt, bwv = self.wload(Wk3[:, :, 256:512], KC, 256)
                    for blk in range(0 if p == 0 else 1, 10):
                        r0, r1 = ABND[blk], ABND[blk + 1]
                        n = r1 - r0
                        hb = bl("h", range(KC), r0, r1)
                        pv, bpv = self.ps_next()
                        self.pe([(pv[0:n, 0:256], h[:, k, r0:r1], wvt[:, k, :], k == 0, k == KC - 1) for k in range(KC)], hb + [bwv], [bpv])
                        self.cp(V[0:n, blk, :], pv[0:n, 0:256], [bpv], [b_V(blk)], eng="act")
                        if blk >= 8:
                            pk, bpk = self.ps_next()
                            self.pe([(pk[0:n, 0:256], h[:, k, r0:r1], wkt[:, k, :], k == 0, k == KC - 1) for k in range(KC)], hb + [bwk], [bpk])
                            sg, bsg, chn = self.stg_next()
                            self.cp(sg[0:n, 0:256], pk[0:n, 0:256], [bpk], [bsg])
                            self.cp(sg[0:n, 256:512], pv[0:n, 0:256], [bpv], [bsg])
                            if blk == 8:
                                self.dma("sp", swk[p], sg[:, 0:256], [bsg], [], chn)
                                self.dma("sp", swv[p], sg[:, 256:512], [bsg], [], chn)
                            else:
                                for s_i in range(NSEQ):
                                    self.dma("sp", sks[p][s_i, 120:128, :], sg[s_i * 8:(s_i + 1) * 8, 0:256], [bsg], [], chn, join=(s_i > 0))
                                    self.dma("sp", svs[p][s_i, 120:128, :], sg[s_i * 8:(s_i + 1) * 8, 256:512], [bsg], [], chn, join=True)

                self.stage("p%dl%dkv" % (p, l))
                dfr = ffn(w_f1g[l], w_f1u[l], w_f1d[l], 0 + l, c0, NT, skip_norm=(l != 2), nxt=(4 + l, c0, NT, False), pre=dfr)
                for s_pf in range(2):
                    mem_prefetch(l, p, s_pf)
                    if l >= 2:
                        swa_prefetch(p, s_pf)
                self.stage("p%dl%df1" % (p, l))
                if l < 2:
                    Wi3 = w_ina[l].rearrange("(k p) n -> p k n", p=128)
                    proj_to_big(Wi3, [(3072 + hh * 128, 8 + hh) for hh in range(4)], c0, NT, pre=dfr)
                    dfr = None
                    temps = [b_tmpc, b_acc, b_up, b_us, b_cs]
                    S.op("dve", lambda e: e.memset(up[:, 0:2], 0.0), [], allbig(range(12, 19)) + temps)
                    tl = tiles(c0, NT)
                    for i in range(KC):
                        wv_, bw = self.wload(Wi3[:, :, 1024 + i * 128:1024 + (i + 1) * 128], KC, 128)
                        for (t0, t1) in tl:
                            n = t1 - t0
                            pt, bpt = self.ps_next()
                            self.pe([(pt[:, 0:n], wv_[:, k, :], h[:, k, t0:t1], k == 0, k == KC - 1) for k in range(KC)], bl("h", range(KC), t0, t1) + [bw], [bpt])
                            self.cp(tmpc[:, t0:t1], pt[:, 0:n], [bpt], [b_tmpc], eng="act")
                        self.cp(us[:, :, 0:2], scp[:, l, i, :].rearrange("p (s r) -> p s r", s=NSEQ), [b_scp], [b_us])
                        if p == 1:
                            self.cp(up[:, c0:c0 + 2], uprev[:, l, i, :], [bb("uprev")], [b_up])
                        wv_, bw = self.wload(Wi3[:, :, 2048 + i * 128:2048 + (i + 1) * 128], KC, 128)
                        for (t0, t1) in tl:
                            n = t1 - t0
                            pt, bpt = self.ps_next()
                            self.pe([(pt[:, 0:n], wv_[:, k, :], h[:, k, t0:t1], k == 0, k == KC - 1) for k in range(KC)], bl("h", range(KC), t0, t1) + [bw], [bpt])
                            e1 = min(t1, CS0)
                            if t0 < e1:
                                self.tt(up[:, 2 + t0:2 + e1], tmpc[:, t0:e1], pt[:, 0:e1 - t0], ALU.mult, [bpt, b_tmpc], [b_up])
                            if t1 > CS0:
                                s0 = max(t0, CS0)
                                self.tt(us[:, :, 2:10], tmpc[:, s0:t1].rearrange("p (s i) -> p s i", s=NSEQ),
                                        pt[:, s0 - t0:t1 - t0].rearrange("p (s i) -> p s i", s=NSEQ), ALU.mult, [bpt, b_tmpc], [b_us])
                        w0, w1, w2 = G(18 + l * 3 + 0, i), G(18 + l * 3 + 1, i), G(18 + l * 3 + 2, i)
                        self.ts(acc[:, c0:CS0], up[:, 2 + c0:2 + CS0], w2, ALU.mult, [b_up, b_gains], [b_acc])
                        self.stt(acc[:, c0:CS0], up[:, 1 + c0:1 + CS0], w1, acc[:, c0:CS0], ALU.mult, ALU.add, [b_up, b_acc, b_gains], [b_acc])
                        self.stt(acc[:, c0:CS0], up[:, c0:CS0], w0, acc[:, c0:CS0], ALU.mult, ALU.add, [b_up, b_acc, b_gains], [b_acc])
                        accs = acc[:, CS0:NT].rearrange("p (s i) -> p s i", s=NSEQ)
                        self.ts(accs, us[:, :, 2:10], w2, ALU.mult, [b_us, b_gains], [b_acc])
                        self.stt(accs, us[:, :, 1:9], w1, accs, ALU.mult, ALU.add, [b_us, b_acc, b_gains], [b_acc])
                        self.stt(accs, us[:, :, 0:8], w0, accs, ALU.mult, ALU.add, [b_us, b_acc, b_gains], [b_acc])
                        self.cp(cs[:, i, 0:16].rearrange("p (s r) -> p s r", s=NSEQ), us[:, :, 8:10], [b_us], [b_cs])
                        self.cp(cs[:, i, 16:18], up[:, CS0:CS0 + 2], [b_up], [b_cs])
                        if p == 0:
                            self.cp(uprev[:, l, i, :], up[:, CS0:CS0 + 2], [b_up], [bb("uprev")])
                        wv_, bw = self.wload(Wi3[:, :, i * 128:(i + 1) * 128], KC, 128)
                        for (t0, t1) in tl:
                            n = t1 - t0
                            pt, bpt = self.ps_next()
                            self.pe([(pt[:, 0:n], wv_[:, k, :], h[:, k, t0:t1], k == 0, k == KC - 1) for k in range(KC)], bl("h", range(KC), t0, t1) + [bw], [bpt])
                            self.tt(big[:, i, t0:t1], acc[:, t0:t1], pt[:, 0:n], ALU.mult, [bpt, b_acc], bl("big", [i], t0, t1))
                    sg, bsg, chn = self.stg_next()
                    for half in range(2):
                        pt, bpt = self.ps_next()
                        self.pe_tr([(pt[0:18, i * 128:(i + 1) * 128], cs[:, half * 4 + i, :], ident[:]) for i in range(4)], [b_cs, b_ident], [bpt])
                        self.cp(sg[0:18, half * 512:(half + 1) * 512], pt[0:18, 0:512], [bpt], [bsg])
                    self.dma("sp", cso[p, l], sg[0:18, :], [bsg], [], chn)
                    S.op("dve", lambda e: e.memset(up[:, 0:1], 0.0), temps, allbig(range(12, 19)) + temps)
                    mem_attn(l, p, c0)
                    dfr = wout(w_outa[l], c0, NT, nxt=(8 + l, c0, NT, False), halo_mask=(p == 0))
                else:
                    jl = l - 2
                    Wi3 = w_inb[jl].rearrange("(k p) n -> p k n", p=128)
                    proj_to_big(Wi3, [(c * 128, c) for c in range(12)], CM0, NT, pre=dfr)
                    dfr = None
                    swa_attn(jl, p)
                    mem_attn(l, p, CM0)
                    dfr = wout(w_outb[jl], CM0, NT, nxt=(8 + l, CM0, NT, False))
                self.stage("p%dl%dmix" % (p, l))
                if l == 0:
                    nx = (1, c0, NT, False)
                elif l == 1:
                    nx = (16, c0, NT, False)
                elif l == 2:
                    nx = (3, CM0, NT, False)
                else:
                    nx = (17, CM0, NT, True)
                dfr = ffn(w_f2g[l], w_f2u[l], w_f2d[l], 8 + l, c0, NT, skip_norm=True, nxt=nx, pre=dfr)
                if l == DEPTH - 1 and dfr is not None:
                    dfr()
                    dfr = None
                self.stage("p%dl%df2" % (p, l))

            for (r0, r1) in [(CM0 + i * 128, min(CM0 + (i + 1) * 128, NT)) for i in range(9)]:
                n = r1 - r0
                sg, bsg, chn = self.stg_next()
                for half in range(2):
                    pt, bpt = self.ps_next()
                    self.pe_tr([(pt[0:n, i * 128:(i + 1) * 128], x[:, half * 4 + i, r0:r1], ident[:]) for i in range(4)],
                               bl("x", range(half * 4, half * 4 + 4), r0, r1) + [b_ident], [bpt])
                    self.cp(sg[0:n, half * 512:(half + 1) * 512], pt[0:n, 0:512], [bpt], [bsg], eng=("act" if half == 0 else "dve"))
                self.dma("sp", yout[p, r0 - CM0:r1 - CM0, :], sg[0:n, :], [bsg], [], chn)


def _bucket_onehot():
    d = np.arange(128)
    exact = 16
    nf = np.maximum(d, 1).astype(np.float32)
    large = exact + (np.log(nf / np.float32(exact)) / np.float32(math.log(128 / exact)) * np.float32(32 - exact)).astype(np.int32)
    large = np.minimum(large, 31)
    bk = np.where(d < exact, d, large)
    oh = np.zeros((32, 128), np.float32)
    oh[bk, d] = 1.0
    return oh


_NC_CACHE = {}


def kernel(x_prompt, x_sample, state_conv, cache_swa_k, cache_swa_v, cache_mem_k, cache_mem_v,
           mem_prompt, ffn1_norm, ffn1_wg, ffn1_wu, ffn1_wd, mix_norm, w_in_a, conv_w, w_out_a,
           kv_norm, w_kv, w_in_b, attn_sinks, rel_bias, w_out_b, mem_norm, w_mem_kv,
           ffn2_norm, ffn2_wg, ffn2_wu, ffn2_wd, final_norm):
    f = lambda a: np.ascontiguousarray(np.asarray(a, dtype=np.float32))
    x_prompt, x_sample, state_conv = f(x_prompt), f(x_sample), f(state_conv)
    cache_swa_k, cache_swa_v = f(cache_swa_k), f(cache_swa_v)
    cache_mem_k, cache_mem_v, mem_prompt = f(cache_mem_k), f(cache_mem_v), f(mem_prompt)
    if "nc" not in _NC_CACHE:
        _NC_CACHE["nc"] = Prog().build()
    nc = _NC_CACHE["nc"]
    in_maps = _make_in_maps(x_prompt, x_sample, state_conv, cache_swa_k, cache_swa_v, cache_mem_k, cache_mem_v,
                            mem_prompt, ffn1_norm, ffn1_wg, ffn1_wu, ffn1_wd, mix_norm, w_in_a, conv_w, w_out_a,
                            kv_norm, w_kv, w_in_b, attn_sinks, rel_bias, w_out_b, mem_norm, w_mem_kv,
                            ffn2_norm, ffn2_wg, ffn2_wu, ffn2_wd, final_norm)
    res = run_bass_kernel_spmd(nc, in_maps, core_ids=list(range(8)))
    return _gather(res.results)


def _make_in_maps(x_prompt, x_sample, state_conv, cache_swa_k, cache_swa_v, cache_mem_k, cache_mem_v,
                  mem_prompt, ffn1_norm, ffn1_wg, ffn1_wu, ffn1_wd, mix_norm, w_in_a, conv_w, w_out_a,
                  kv_norm, w_kv, w_in_b, attn_sinks, rel_bias, w_out_b, mem_norm, w_mem_kv,
                  ffn2_norm, ffn2_wg, ffn2_wu, ffn2_wd, final_norm, cores=range(8)):
    f = lambda a: np.ascontiguousarray(np.asarray(a, dtype=np.float32))

    vecs = np.concatenate([f(ffn1_norm), f(mix_norm), f(ffn2_norm), f(mem_norm), f(kv_norm)[None], f(final_norm)[None],
                           f(conv_w).reshape(6, D)], axis=0)
    sk = f(attn_sinks).reshape(2, 4, 2, 2)
    sinks = np.ascontiguousarray(sk.transpose(3, 0, 1, 2).reshape(2, 16))
    ident = np.eye(128, dtype=np.float32)
    jm = np.zeros((128, 383), np.float32)
    dd = np.arange(128)
    jm[dd, 255 - dd] = 1.0
    oh = _bucket_onehot()
    bd = np.kron(np.eye(8, dtype=np.float32), np.ones((8, 8), np.float32))
    shared = dict(vecs=vecs, f1g=f(ffn1_wg), f1u=f(ffn1_wu), f1d=f(ffn1_wd), f2g=f(ffn2_wg), f2u=f(ffn2_wu), f2d=f(ffn2_wd),
                  wina=f(w_in_a), wouta=f(w_out_a), winb=f(w_in_b), woutb=f(w_out_b), wkv=f(w_kv), wmkv=f(w_mem_kv),
                  relb=f(rel_bias), sinks=sinks, ident=ident, jmat=jm, ohot=oh, bdiag=bd)
    in_maps = []
    for c in cores:
        b, q = c // 4, c % 4
        xin = np.zeros((NPASS, NT, D), np.float32)
        hv = np.zeros((NPASS, 128, 64), np.float32)
        for p in range(NPASS):
            t0 = q * 2048 + p * MAIN
            lo = t0 - HALO
            if lo >= 0:
                xin[p, 0:HALO + MAIN] = x_prompt[b, lo:t0 + MAIN]
                hv[p] = 1.0
            else:
                xin[p, HALO:HALO + MAIN] = x_prompt[b, t0:t0 + MAIN]
            s0 = c * 16 + p * NSEQ
            xin[p, CS0:NT] = x_sample[s0:s0 + NSEQ].reshape(SAMP, D)
        seqs = [slice(c * 16 + p * NSEQ, c * 16 + (p + 1) * NSEQ) for p in range(NPASS)]
        m = dict(shared)
        m["xin"] = xin
        m["hv"] = hv
        m["scv"] = np.stack([state_conv[:, s].reshape(2, 2 * NSEQ, D) for s in seqs], axis=0)
        m["ck"] = np.stack([cache_swa_k[s].reshape(NSEQ, 128, 256) for s in seqs], axis=0)
        m["cv"] = np.stack([cache_swa_v[s].reshape(NSEQ, 128, 256) for s in seqs], axis=0)
        m["cmk"] = np.stack([cache_mem_k[:, s].reshape(DEPTH, NSEQ, 256, 512) for s in seqs], axis=1)
        m["cmv"] = np.stack([cache_mem_v[:, s].reshape(DEPTH, NSEQ, 256, 512) for s in seqs], axis=1)
        m["mem"] = mem_prompt[b]
        in_maps.append({k: np.ascontiguousarray(v) for k, v in m.items()})
    return in_maps


def _gather(R):
    y_prompt = np.zeros((2, 8192, D), np.float32)
    y_sample = np.zeros((128, 8, D), np.float32)
    csp = np.zeros((2, 2, 2, D), np.float32)
    css = np.zeros((2, 128, 2, D), np.float32)
    swkp = np.zeros((2, 128, 4, 64), np.float32)
    swvp = np.zeros((2, 128, 4, 64), np.float32)
    swks = np.zeros((128, 128, 4, 64), np.float32)
    swvs = np.zeros((128, 128, 4, 64), np.float32)
    mkp = np.zeros((DEPTH, 2, 256, 4, 128), np.float32)
    mvp = np.zeros((DEPTH, 2, 256, 4, 128), np.float32)
    for c in range(8):
        b, q = c // 4, c % 4
        r = R[c]
        for p in range(NPASS):
            t0 = q * 2048 + p * MAIN
            y_prompt[b, t0:t0 + MAIN] = r["yout"][p, 0:MAIN]
            s0 = c * 16 + p * NSEQ
            y_sample[s0:s0 + NSEQ] = r["yout"][p, MAIN:NB].reshape(NSEQ, 8, D)
            css[:, s0:s0 + NSEQ] = r["cso"][p, :, 0:16].reshape(2, NSEQ, 2, D)
            swks[s0:s0 + NSEQ] = r["sks"][p].reshape(NSEQ, 128, 4, 64)
            swvs[s0:s0 + NSEQ] = r["svs"][p].reshape(NSEQ, 128, 4, 64)
        if q == 3:
            csp[:, b] = r["cso"][1, :, 16:18]
            swkp[b] = r["swk"][1].reshape(128, 4, 64)
            swvp[b] = r["swv"][1].reshape(128, 4, 64)
        if q == 0:
            mkp[:, b] = r["mko"].reshape(DEPTH, 256, 4, 128)
            mvp[:, b] = r["mvo"].reshape(DEPTH, 256, 4, 128)
    return (y_prompt, y_sample, csp, css, swkp, swvp, swks, swvs, mkp, mvp)
```

```python
import math
from contextlib import ExitStack

import numpy as np
import concourse.bass as bass
import concourse.mybir as mybir
from concourse.bass_utils import run_bass_kernel_spmd

F32 = mybir.dt.float32
BF16 = mybir.dt.bfloat16
AF = mybir.ActivationFunctionType
ALU = mybir.AluOpType

D = 1024
KC = 8
FF = 2816
FC = 22
DEPTH = 4
NPASS = 2
HALO = 132
MAIN = 1024
SAMP = 64
NSEQ = 8
NT = HALO + MAIN + SAMP
CM0 = HALO
CS0 = HALO + MAIN
NB = NT - HALO
ABND = [4] + [132 + 128 * i for i in range(9)] + [NT]
BND = sorted(set([0] + ABND + [407, 814, 495, 858, 410, 815]))
NBLK = len(BND) - 1
RMS_EPS = 1e-5
NW = 5
WSLOT = 2816
SEM_ROLL = 30000
DEFER_NORM_TAIL = True
ENGS = ("pe", "act", "dve", "pool", "sp")


class Buf:
    __slots__ = ("w", "r", "x")

    def __init__(self, excl=False):
        self.w = None
        self.r = []
        self.x = excl


class Chan:
    __slots__ = ("sem", "count", "last")

    def __init__(self, sem):
        self.sem = sem
        self.count = 0
        self.last = None


class Op:
    __slots__ = ("eng", "fn", "deps", "chan", "sig", "need")

    def __init__(self, eng, fn, chan):
        self.eng = eng
        self.fn = fn
        self.deps = []
        self.chan = chan
        self.sig = None
        self.need = False


class Sched:
    def __init__(self, nc, stack):
        self.nc = nc
        self.stack = stack
        self.ops = {e: [] for e in ENGS}
        self.all = []
        self.chans = []

    def new_chan(self, name):
        sem = self.stack.enter_context(self.nc.semaphore(name))
        c = Chan(sem)
        self.chans.append(c)
        return c

    def op(self, eng, fn, reads=(), writes=(), chan=None, join=False):
        o = Op(eng, fn, chan)
        deps = {}
        xr = [b for b in reads if b.x]
        if xr:
            reads = [b for b in reads if not b.x]
            writes = list(writes) + xr
        for b in reads:
            if b.w is not None:
                deps[id(b.w)] = b.w
        for b in writes:
            if b.w is not None:
                deps[id(b.w)] = b.w
            for r in b.r:
                deps[id(r)] = r
        if chan is not None and chan.last is not None:
            if join:
                deps.pop(id(chan.last), None)
            else:
                deps[id(chan.last)] = chan.last
        for b in reads:
            b.r.append(o)
        for b in writes:
            b.w = o
            b.r = []
        if chan is not None:
            chan.last = o
        dl = []
        for d in deps.values():
            if d is o:
                continue
            if d.chan is None and chan is None and d.eng == "pe" and eng == "pe":
                continue
            d.need = True
            dl.append(d)
        o.deps = dl
        self.ops[eng].append(o)
        self.all.append(o)
        return o

    def emit(self):
        nc = self.nc
        for o in self.all:
            if o.chan is not None:
                o.chan.count += 16
                o.sig = (o.chan.sem, o.chan.count, 16)
        for e in ENGS:
            cur = None
            cnt = 0
            k = 0
            for o in self.ops[e]:
                if o.chan is not None:
                    continue
                if o.need:
                    if cur is None or cnt >= SEM_ROLL:
                        cur = self.stack.enter_context(nc.semaphore("s_%s_%d" % (e, k)))
                        k += 1
                        cnt = 0
                    cnt += 1
                    o.sig = (cur, cnt, 1)
        finals = [(c.sem, c.count) for c in self.chans if c.count > 0]
        ops = self.ops

        def run(e, eng):
            waited = {}
            for o in ops[e]:
                for d in o.deps:
                    sem, val, _ = d.sig
                    if waited.get(sem.num, 0) < val:
                        eng.wait_ge(sem, val)
                        waited[sem.num] = val
                ins = o.fn(eng)
                if o.sig is not None:
                    ins.then_inc(o.sig[0], o.sig[2])
            if e == "sp":
                for sem, val in finals:
                    if waited.get(sem.num, 0) < val:
                        eng.wait_ge(sem, val)

        with nc.Block() as block:
            @block.tensor
            def _(eng):
                run("pe", eng)

            @block.scalar
            def _(eng):
                run("act", eng)

            @block.vector
            def _(eng):
                run("dve", eng)

            @block.gpsimd
            def _(eng):
                run("pool", eng)

            @block.sync
            def _(eng):
                run("sp", eng)


def tiles(c0, c1, mx=512):
    n = c1 - c0
    k = (n + mx - 1) // mx
    base = n // k
    rem = n % k
    out = []
    c = c0
    for i in range(k):
        w = base + (1 if i < rem else 0)
        out.append((c, c + w))
        c += w
    return out


def blks(c0, c1):
    return [i for i in range(NBLK) if BND[i] < c1 and BND[i + 1] > c0]


class _Stop(Exception):
    pass


class Prog:
    def __init__(self, stop=None):
        self.stop = stop
        self.nc = nc = bass.Bass("TRN2", target_bir_lowering=False)
        self.st = ExitStack()
        self.S = Sched(nc, self.st)
        self.B = {}
        self.dr = {}

    def bb(self, *key):
        b = self.B.get(key)
        if b is None:
            b = self.B[key] = Buf(excl=(key[0] == "ps"))
        return b

    def bl(self, name, chs, c0, c1):
        return [self.bb(name, ch, b) for ch in chs for b in blks(c0, c1)]

    def din(self, name, shape, dt=F32):
        self.dr[name] = self.nc.dram_tensor(name, list(shape), dt, kind="ExternalInput").ap()
        return self.dr[name]

    def dout(self, name, shape, dt=F32):
        self.dr[name] = self.nc.dram_tensor(name, list(shape), dt, kind="ExternalOutput").ap()
        return self.dr[name]

    def dscr(self, name, shape, dt):
        self.dr[name] = self.nc.dram_tensor(name, list(shape), dt, kind="Internal").ap()
        return self.dr[name]

    def sb(self, name, shape, dt):
        return self.st.enter_context(self.nc.sbuf_tensor(name, list(shape), dt))

    def ring(self, name, n):
        return {"i": 0, "n": n, "name": name}

    def nxt(self, rg):
        i = rg["i"]
        rg["i"] = (i + 1) % rg["n"]
        return i

    def pe(self, lst, reads, writes):
        lst = list(lst)

        def fn(e):
            ins = None
            for (o, l, r, s, t) in lst:
                ins = e.matmul(o, lhsT=l, rhs=r, start=s, stop=t)
            return ins
        self.S.op("pe", fn, reads, writes)

    def pe_tr(self, lst, reads, writes):
        lst = list(lst)

        def fn(e):
            ins = None
            for (o, i, idn) in lst:
                ins = e.transpose(o, i, idn)
            return ins
        self.S.op("pe", fn, reads, writes)

    def act(self, out, in_, func, reads, writes, scale=None, bias=None):
        kw = {}
        if scale is not None:
            kw["scale"] = scale
        if bias is not None:
            kw["bias"] = bias
        self.S.op("act", lambda e: e.activation(out=out, in_=in_, func=func, **kw), reads, writes)

    def tt(self, out, in0, in1, op, reads, writes):
        self.S.op("dve", lambda e: e.tensor_tensor(out=out, in0=in0, in1=in1, op=op), reads, writes)

    def stt(self, out, in0, scalar, in1, op0, op1, reads, writes):
        self.S.op("dve", lambda e: e.scalar_tensor_tensor(out=out, in0=in0, scalar=scalar, in1=in1, op0=op0, op1=op1), reads, writes)

    def ts(self, out, in0, s1, op0, reads, writes):
        self.S.op("dve", lambda e: e.tensor_scalar(out=out, in0=in0, scalar1=s1, scalar2=None, op0=op0), reads, writes)

    def cp(self, out, in_, reads, writes, eng="dve"):
        if eng == "dve":
            self.S.op("dve", lambda e: e.tensor_copy(out=out, in_=in_), reads, writes)
        else:
            self.S.op("act", lambda e: e.copy(out=out, in_=in_), reads, writes)

    def dma(self, q, out, in_, reads, writes, chan, join=False, **kw):
        self.S.op(q, lambda e: e.dma_start(out=out, in_=in_, **kw), reads, writes, chan=chan, join=join)

    def ps_next(self):
        i = self.nxt(self.psr)
        return self.ps[i], self.bb("ps", i)

    def t32_next(self):
        i = self.nxt(self.t32r)
        return self.t32[i], self.bb("t32", i)

    def t16_next(self):
        i = self.nxt(self.t16r)
        return self.t16[i], self.bb("t16", i)

    def stg_next(self):
        i = self.nxt(self.stgr)
        return self.stg[i], self.bb("stg", i), self.stgc[i]

    def wload(self, src, a, b):
        i = self.nxt(self.wr)
        v = self.wt[i][:, 0:a * b].rearrange("p (a b) -> p a b", a=a)
        bf = self.bb("w", i)
        self.dma("pool", v, src, [], [bf], self.wc[i])
        return v, bf

    def stage(self, name):
        if self.stop == name:
            raise _Stop()

    def build(self):
        try:
            self._build()
        except _Stop:
            x, h = self.x, self.h
            xd = self.dout("xdump", [128, KC, NT])
            hd = self.dout("hdump", [128, KC, NT], BF16)
            bd = self.dout("bigdump", [128, FC, NT], BF16)
            allb = lambda nm, n: [self.bb(nm, ch, b) for ch in range(n) for b in range(NBLK)]
            c1, c2, c3 = [self.S.new_chan("dbg%d" % i) for i in range(3)]
            self.dma("sp", xd, x[:], allb("x", KC), [], c1)
            self.dma("sp", hd, h[:], allb("h", KC), [], c2)
            self.dma("sp", bd, self.big, allb("big", FC) + [self.bb(k) for k in ("tmpc", "acc", "up", "us", "cs")], [], c3)
        self.S.emit()
        return self.nc

    def _build(self):
        nc = self.nc
        S = self.S
        xin = self.din("xin", [NPASS, NT, D])
        scv = self.din("scv", [NPASS, 2, 2 * NSEQ, D])
        ck = self.din("ck", [NPASS, NSEQ, 128, 256])
        cv = self.din("cv", [NPASS, NSEQ, 128, 256])
        cmk = self.din("cmk", [DEPTH, NPASS, NSEQ, 256, 512])
        cmv = self.din("cmv", [DEPTH, NPASS, NSEQ, 256, 512])
        mem = self.din("mem", [256, D])
        vecs = self.din("vecs", [24, D])
        w_f1g = self.din("f1g", [DEPTH, D, FF])
        w_f1u = self.din("f1u", [DEPTH, D, FF])
        w_f1d = self.din("f1d", [DEPTH, FF, D])
        w_f2g = self.din("f2g", [DEPTH, D, FF])
        w_f2u = self.din("f2u", [DEPTH, D, FF])
        w_f2d = self.din("f2d", [DEPTH, FF, D])
        w_ina = self.din("wina", [2, D, 3584])
        w_outa = self.din("wouta", [2, 1536, D])
        w_inb = self.din("winb", [2, D, 1536])
        w_outb = self.din("woutb", [2, 1536, D])
        w_kv = self.din("wkv", [D, 512])
        w_mkv = self.din("wmkv", [DEPTH, D, 1024])
        relb = self.din("relb", [32, 16])
        sinks = self.din("sinks", [2, 16])
        ident_d = self.din("ident", [128, 128])
        J_d = self.din("jmat", [128, 383])
        OH_d = self.din("ohot", [32, 128])
        BD_d = self.din("bdiag", [64, 64])
        hv_d = self.din("hv", [NPASS, 128, 64])

        yout = self.dout("yout", [NPASS, NB, D])
        cso = self.dout("cso", [NPASS, 2, 18, D])
        swk = self.dout("swk", [NPASS, 128, 256])
        swv = self.dout("swv", [NPASS, 128, 256])
        sks = self.dout("sks", [NPASS, NSEQ, 128, 256])
        svs = self.dout("svs", [NPASS, NSEQ, 128, 256])
        mko = self.dout("mko", [DEPTH, 256, 512])
        mvo = self.dout("mvo", [DEPTH, 256, 512])
        mkt_s = self.dscr("mkt_s", [DEPTH, 128, 1024], BF16)
        mv_s = self.dscr("mv_s", [DEPTH, 2, 128, 512], BF16)

        self.x = x = self.sb("x", [128, KC, NT], F32)
        self.h = h = self.sb("h", [128, KC, NT], BF16)
        bigT = self.sb("big", [128, FC * NT], BF16)
        self.big = big = bigT[:].rearrange("p (c t) -> p c t", c=FC)
        big32 = bigT.bitcast(F32)
        self.wt = [self.sb("w%d" % i, [128, WSLOT], BF16) for i in range(NW)]
        self.wc = [S.new_chan("wc%d" % i) for i in range(NW)]
        self.wr = self.ring("w", NW)
        self.KT = KT = self.sb("KT", [128, 4, NT - 4], BF16)
        self.V = V = self.sb("V", [128, 10, 256], BF16)
        self.E = E = self.sb("E", [128, 2, 16, 128], BF16)
        self.En = En = self.sb("En", [64, 2, 512], BF16)
        self.mKT = mKT = self.sb("mKT", [128, 4, 256], BF16)
        self.mV = mV = self.sb("mV", [128, 2, 512], BF16)
        kmc = [self.sb("kmc%d" % i, [128, 2, 512], BF16) for i in range(2)]
        vmc = [self.sb("vmc%d" % i, [128, 2, 512], BF16) for i in range(2)]
        kmT = self.sb("kmT", [128, 4, 2, 128], BF16)
        kc16 = [self.sb("kc16_%d" % i, [128, 4, 2, 64], BF16) for i in range(2)]
        kcT = [self.sb("kcT%d" % i, [128, 4, 128], BF16) for i in range(2)]
        vc16 = [self.sb("vc16_%d" % i, [128, 256], BF16) for i in range(2)]
        self.stg = [self.sb("stg%d" % i, [128, 1024], F32) for i in range(2)]
        self.stgc = [S.new_chan("stgc%d" % i) for i in range(2)]
        self.stgr = self.ring("stg", 2)
        self.t32 = [self.sb("t32_%d" % i, [128, 512], F32) for i in range(3)]
        self.t32r = self.ring("t32", 3)
        self.t16 = [self.sb("t16_%d" % i, [128, 512], BF16) for i in range(4)]
        self.t16r = self.ring("t16", 4)
        ident = self.sb("identf", [128, 128], F32)
        identb = self.sb("identb", [128, 128], BF16)
        ones = self.sb("ones", [128, 128], BF16)
        onesD = self.sb("onesD", [128, 128], BF16)
        hvt = self.sb("hvt", [128, 64], BF16)
        hvf = self.sb("hvf", [128, 1], F32)
        gains = self.sb("gains", [128, 192], F32)
        esb = self.sb("esb", [128, 16], F32)
        scp = self.sb("scp", [128, 2, KC, 16], F32)
        uprev = self.sb("uprev", [128, 2, KC, 2], F32)
        self.ps = [self.st.enter_context(nc.psum_tensor("ps%d" % i, [128, 512], F32)) for i in range(8)]
        self.psr = self.ring("ps", 6)
        ps = self.ps
        ps6, ps7 = ps[6], ps[7]
        bps6, bps7 = self.bb("ps", 6), self.bb("ps", 7)

        misc = [S.new_chan("mc%d" % i) for i in range(6)]
        mr = self.ring("misc", 6)

        def mchan():
            return misc[self.nxt(mr)]

        pmisc = [S.new_chan("pmc%d" % i) for i in range(2)]
        pmr = self.ring("pmisc", 2)

        def pchan():
            return pmisc[self.nxt(pmr)]

        bb = self.bb
        bl = self.bl
        b_ident, b_identb, b_ones, b_onesD = bb("ident"), bb("identb"), bb("ones"), bb("onesD")
        b_gains, b_esb, b_E, b_En, b_hv = bb("gains"), bb("esb"), bb("E"), bb("En"), bb("hv")
        allbig = lambda chs: [bb("big", ch, b) for ch in chs for b in range(NBLK)]

        tmpc = big32[:, 7320:7320 + NT]
        acc = big32[:, 8540:8540 + NT]
        up = big32[:, 9760:9760 + 2 + CS0]
        us = big32[:, 10920:11000].rearrange("p (s t) -> p s t", s=NSEQ)
        cs = big32[:, 11000:11144].rearrange("p (c t) -> p c t", c=KC)
        b_tmpc, b_acc, b_up, b_us, b_cs = bb("tmpc"), bb("acc"), bb("up"), bb("us"), bb("cs")

        self.stage("pro0")
        self.dma("sp", ident[:], ident_d, [], [b_ident], mchan())
        self.cp(identb[:], ident[:], [b_ident], [b_identb])
        S.op("dve", lambda e: e.memset(ones[:], 1.0), [], [b_ones])
        S.op("dve", lambda e: e.memset(onesD[:], 1.0 / D), [], [b_onesD])
        vrows = vecs.rearrange("v (c p) -> (v c) p", p=128)
        for half in range(2):
            sg, bsg, chn = self.stg_next()
            self.dma("sp", sg[0:96, 0:128], vrows[half * 96:(half + 1) * 96, :], [], [bsg], chn)
            pt, bpt = self.ps_next()
            self.pe_tr([(pt[:, 0:96], sg[0:96, 0:128], ident[0:96, 0:96])], [bsg, b_ident], [bpt])
            self.cp(gains[:, half * 96:(half + 1) * 96], pt[:, 0:96], [bpt], [b_gains])
        G = lambda v, c: gains[:, v * 8 + c: v * 8 + c + 1]
        for par in range(2):
            self.dma("sp", esb[par * 64:(par + 1) * 64, :], sinks[par:par + 1, :].partition_broadcast(64), [], [b_esb], mchan())
        self.act(esb[:], esb[:], AF.Exp, [b_esb], [b_esb])

        self.stage("pro1")
        Jt = big32[:, 0:383]
        OHt = big32[0:32, 400:528]
        BDt = big32[0:64, 600:664]
        ttb = big32[:, 700:716]
        rbt = big32[0:32, 720:736]
        b_pro = bb("pro")
        Jb = bigT[:, 1600:1983]
        ttb16 = bigT[:, 2000:2016]
        self.dma("pool", Jb, J_d, [], [b_pro], pchan())
        self.dma("sp", OHt, OH_d, [], [b_pro], mchan())
        self.dma("sp", BDt, BD_d, [], [b_pro], mchan())
        self.dma("sp", rbt, relb, [], [b_pro], mchan())
        pt, bpt = self.ps_next()
        self.pe([(pt[:, 0:16], OHt, rbt, True, True)], [b_pro], [bpt])
        b_tt = bb("ttb")
        self.act(ttb16, pt[:, 0:16], AF.Exp, [bpt, b_pro], [b_tt])
        def e_group(j, qb):
            pt, bpt = self.ps_next()
            lst = []
            for qi in range(32):
                q = qb * 32 + qi
                off = (127 - q) if j == 0 else (255 - q)
                lst.append((pt[:, qi * 16:(qi + 1) * 16], Jb[:, off:off + 128], ttb16, True, True))
            self.pe(lst, [b_pro, b_tt], [bpt])
            self.cp(E[:, j, :, qb * 32:(qb + 1) * 32], pt[:, 0:512].rearrange("p (q h) -> p h q", h=16), [bpt], [b_E])
        e_groups = [(j, qb) for j in range(2) for qb in range(4)]
        self.stage("pro3")
        xm = x[:, :, 0:256]
        hm = h[:, :, 0:256]
        b_xm, b_hm = bb("xm"), bb("hm")
        for rb_ in range(2):
            sg, bsg, chn = self.stg_next()
            self.dma("sp", sg[:], mem[rb_ * 128:(rb_ + 1) * 128, :], [], [bsg], chn)
            for half in range(2):
                pt, bpt = self.ps_next()
                self.pe_tr([(pt[:, i * 128:(i + 1) * 128], sg[:, (half * 4 + i) * 128:(half * 4 + i + 1) * 128], ident[:]) for i in range(4)],
                           [bsg, b_ident], [bpt])
                self.cp(x[:, half * 4:half * 4 + 4, rb_ * 128:(rb_ + 1) * 128], pt[:, 0:512].rearrange("p (c t) -> p c t", c=4), [bpt], [b_xm], eng="act")
        self.stage("pro4")
        for m in range(KC):
            self.act(hm[:, m, :], xm[:, m, :], AF.Square, [b_xm], [b_hm])
        pt, bpt = self.ps_next()
        self.pe([(pt[:, 0:256], onesD[:], hm[:, m, :], m == 0, m == KC - 1) for m in range(KC)], [b_hm, b_onesD], [bpt])
        rm, brm = self.t32_next()
        self.act(rm[:, 0:256], pt[:, 0:256], AF.Ln, [bpt], [brm], bias=RMS_EPS)
        self.act(rm[:, 0:256], rm[:, 0:256], AF.Exp, [brm], [brm], scale=-0.5)
        self.stage("pro5")
        for l in range(DEPTH):
            b_msc = bb("mscr", l)
            if l == 1:
                self.stage("pro8")
            hm = h[:, :, (l % 2) * 256:(l % 2 + 1) * 256]
            b_hm = bb("hm", l % 2)
            for m in range(KC):
                self.stt(hm[:, m, :], xm[:, m, :], G(12 + l, m), rm[:, 0:256], ALU.mult, ALU.mult, [b_xm, brm, b_gains], [b_hm])
            Wm = w_mkv[l].rearrange("(k p) n -> p k n", p=128)
            if l == 0:
                self.stage("pro6")
            tk, btk = self.t16_next()
            tk2, btk2 = self.t16_next()
            for hh in range(4):
                wv_, bw = self.wload(Wm[:, :, hh * 128:(hh + 1) * 128], KC, 128)
                pt, bpt = self.ps_next()
                self.pe([(pt[:, 0:256], wv_[:, k, :], hm[:, k, :], k == 0, k == KC - 1) for k in range(KC)], [bw, b_hm], [bpt])
                dst = (tk if hh < 2 else tk2)[:, (hh % 2) * 256:(hh % 2 + 1) * 256]
                self.cp(dst, pt[:, 0:256], [bpt], [btk if hh < 2 else btk2], eng="act")
            self.dma("sp", mkt_s[l][:, 0:512], tk[:], [btk], [b_msc], mchan())
            self.dma("sp", mkt_s[l][:, 512:1024], tk2[:], [btk2], [b_msc], mchan())
            if l == 0:
                self.stage("pro7")
            sgs = [self.stg_next() for _ in range(2)]
            tv, btv = self.t16_next()
            tv2, btv2 = self.t16_next()
            for cg in range(4):
                wv_, bw = self.wload(Wm[:, :, cg * 256:(cg + 1) * 256], KC, 256)
                for mc in range(2):
                    pt, bpt = self.ps_next()
                    self.pe([(pt[:, 0:256], hm[:, k, mc * 128:(mc + 1) * 128], wv_[:, k, :], k == 0, k == KC - 1) for k in range(KC)], [bw, b_hm], [bpt])
                    sg, bsg, chn = sgs[mc]
                    self.cp(sg[:, cg * 256:(cg + 1) * 256], pt[:, 0:256], [bpt], [bsg], eng="act")
                    if cg >= 2:
                        tvv, btvv = (tv, btv) if mc == 0 else (tv2, btv2)
                        self.cp(tvv[:, (cg - 2) * 256:(cg - 1) * 256], pt[:, 0:256], [bpt], [btvv])
            for mc in range(2):
                sg, bsg, chn = sgs[mc]
                self.dma("sp", mko[l][mc * 128:(mc + 1) * 128, :], sg[:, 0:512], [bsg], [], chn)
                self.dma("sp", mvo[l][mc * 128:(mc + 1) * 128, :], sg[:, 512:1024], [bsg], [], chn)
            self.dma("sp", mv_s[l][0], tv[:], [btv], [b_msc], mchan())
            self.dma("sp", mv_s[l][1], tv2[:], [btv2], [b_msc], mchan())
            for _ in range(2):
                e_group(*e_groups.pop(0))
        self.stage("pro2")
        Ev = E[:].rearrange("p j (a par) q -> p j a par q", par=2)
        for par in range(2):
            self.tt(En[:, par, :].rearrange("p (a q) -> p a q", a=8), Ev[0:64, 1, :, par, 0:64],
                    BDt.unsqueeze(1).to_broadcast([64, 8, 64]), ALU.mult, [b_E, b_pro], [b_En])

        S.op("dve", lambda e: e.memset(hvt[0:1, 0:1], 0.0), [b_pro, b_tt, b_xm, bb("hm"), bb("hm", 0), bb("hm", 1), brm],
             allbig(range(FC)) + bl("x", range(KC), 0, NT) + bl("h", range(KC), 0, NT) + [b_hv])

        self.stage("pro")
        def norm_sq(t0, t1):
            for m in range(KC):
                self.act(h[:, m, t0:t1], x[:, m, t0:t1], AF.Square, bl("x", [m], t0, t1), bl("h", [m], t0, t1))

        def norm_rest(gv, t0, t1, final=False):
            n = t1 - t0
            pt, bpt = self.ps_next()
            self.pe([(pt[:, 0:n], onesD[:], h[:, m, t0:t1], m == 0, m == KC - 1) for m in range(KC)],
                    bl("h", range(KC), t0, t1) + [b_onesD], [bpt])
            rs, brs = self.t32_next()
            self.act(rs[:, 0:n], pt[:, 0:n], AF.Ln, [bpt], [brs], bias=RMS_EPS)
            self.act(rs[:, 0:n], rs[:, 0:n], AF.Exp, [brs], [brs], scale=-0.5)
            for m in range(KC):
                if final:
                    self.stt(x[:, m, t0:t1], x[:, m, t0:t1], G(gv, m), rs[:, 0:n], ALU.mult, ALU.mult,
                             bl("x", [m], t0, t1) + [brs, b_gains], bl("x", [m], t0, t1))
                else:
                    self.stt(h[:, m, t0:t1], x[:, m, t0:t1], G(gv, m), rs[:, 0:n], ALU.mult, ALU.mult,
                             bl("x", [m], t0, t1) + [brs, b_gains], bl("h", [m], t0, t1))

        def rmsnorm(gv, c0, c1, final=False):
            for (t0, t1) in tiles(c0, c1):
                norm_sq(t0, t1)
                norm_rest(gv, t0, t1, final)

        def out_proj(W3, nk, src_chunks, c0, c1, scale, nxt, halo_mask=False):
            tl = tiles(c0, c1)

            def group(m, wv_, bw, t0, t1):
                n = t1 - t0
                py, bpy = self.ps_next()
                self.pe([(py[:, 0:n], wv_[:, j, :], big[:, src_chunks[j], t0:t1], j == 0, j == nk - 1) for j in range(nk)],
                        bl("big", src_chunks, t0, t1) + [bw], [bpy])
                xb = bl("x", [m], t0, t1)
                if scale == 1.0:
                    self.tt(x[:, m, t0:t1], py[:, 0:n], x[:, m, t0:t1], ALU.add, [bpy] + xb, xb)
                else:
                    self.stt(x[:, m, t0:t1], py[:, 0:n], scale, x[:, m, t0:t1], ALU.mult, ALU.add, [bpy] + xb, xb)
                if halo_mask and t0 < HALO:
                    e1 = min(t1, HALO)
                    self.ts(x[:, m, t0:e1], x[:, m, t0:e1], hvf[:, 0:1], ALU.mult, bl("x", [m], t0, e1) + [bb("hvf")], bl("x", [m], t0, e1))
                if nxt is not None:
                    s0 = max(t0, nxt[1])
                    if s0 < t1:
                        self.act(h[:, m, s0:t1], x[:, m, s0:t1], AF.Square, bl("x", [m], s0, t1), bl("h", [m], s0, t1))

            for m in range(KC - 2):
                wv_, bw = self.wload(W3[:, :, m * 128:(m + 1) * 128], nk, 128)
                for (t0, t1) in tl:
                    group(m, wv_, bw, t0, t1)
            w6, b6 = self.wload(W3[:, :, 6 * 128:7 * 128], nk, 128)
            w7, b7 = self.wload(W3[:, :, 7 * 128:8 * 128], nk, 128)
            ntl = tiles(nxt[1], nxt[2]) if nxt is not None else []
            rest_done = 0
            prev_end = None
            for (t0, t1) in tl:
                group(6, w6, b6, t0, t1)
                group(7, w7, b7, t0, t1)
                if nxt is not None and prev_end is not None:
                    while rest_done < len(ntl) and ntl[rest_done][1] <= prev_end:
                        norm_rest(nxt[0], ntl[rest_done][0], ntl[rest_done][1], nxt[3])
                        rest_done += 1
                prev_end = t1
            deferred = None
            if nxt is not None:
                while rest_done < len(ntl):
                    norm_rest(nxt[0], ntl[rest_done][0], ntl[rest_done][1], nxt[3])
                    rest_done += 1
            return deferred

        def ffn(Wg, Wu, Wd, gv, c0, c1, skip_norm=False, nxt=None, pre=None):
            if not skip_norm:
                rmsnorm(gv, c0, c1)
            npre = 0
            tl = tiles(c0, c1)
            Wg3 = Wg.rearrange("(k p) n -> p k n", p=128)
            Wu3 = Wu.rearrange("(k p) n -> p k n", p=128)
            Wd3 = Wd.rearrange("(j p) n -> p j n", p=128)
            for jp in range(FC // 2):
                wg, bwg = self.wload(Wg3[:, :, jp * 256:(jp + 1) * 256], KC, 256)
                wu, bwu = self.wload(Wu3[:, :, jp * 256:(jp + 1) * 256], KC, 256)
                if jp == 0 and len(tl) > 1:
                    order = [(jj, t) for t in tl[:-1] for jj in range(2)] + [(jj, tl[-1]) for jj in range(2)]
                else:
                    order = [(jj, t) for jj in range(2) for t in tl]
                for gi, (jj, (t0, t1)) in enumerate(order):
                    j = jp * 2 + jj
                    if pre is not None and jp == 0 and (gi == 2 * (len(tl) - 1) or len(tl) == 1):
                        pre()
                        pre = None
                    if True:
                        n = t1 - t0
                        hb = bl("h", range(KC), t0, t1)
                        pg, bpg = self.ps_next()
                        self.pe([(pg[:, 0:n], wg[:, k, jj * 128:(jj + 1) * 128], h[:, k, t0:t1], k == 0, k == KC - 1) for k in range(KC)], hb + [bwg], [bpg])
                        pu, bpu = self.ps_next()
                        self.pe([(pu[:, 0:n], wu[:, k, jj * 128:(jj + 1) * 128], h[:, k, t0:t1], k == 0, k == KC - 1) for k in range(KC)], hb + [bwu], [bpu])
                        sg, bsg = self.t32_next()
                        self.act(sg[:, 0:n], pg[:, 0:n], AF.Silu, [bpg], [bsg])
                        self.tt(big[:, j, t0:t1], sg[:, 0:n], pu[:, 0:n], ALU.mult, [bsg, bpu], bl("big", [j], t0, t1))
            return out_proj(Wd3, FC, list(range(FC)), c0, c1, 0.5, nxt)

        def wout(Wo, c0, c1, nxt=None, halo_mask=False):
            return out_proj(Wo.rearrange("(k p) n -> p k n", p=128), 12, list(range(12)), c0, c1, 1.0, nxt, halo_mask)

        def proj_group(wv_, bw, dst_chunk, t0, t1):
            n = t1 - t0
            pt, bpt = self.ps_next()
            self.pe([(pt[:, 0:n], wv_[:, k, :], h[:, k, t0:t1], k == 0, k == KC - 1) for k in range(KC)],
                    bl("h", range(KC), t0, t1) + [bw], [bpt])
            self.cp(big[:, dst_chunk, t0:t1], pt[:, 0:n], [bpt], bl("big", [dst_chunk], t0, t1), eng="act")

        def proj_to_big(W3, specs, c0, c1, pre=None):
            tl = tiles(c0, c1)
            specs = list(specs)
            if len(specs) >= 2 and len(tl) > 1:
                ws = [self.wload(W3[:, :, wc:wc + 128], KC, 128) for (wc, _) in specs[:2]]
                for t in tl[:-1]:
                    for i in range(2):
                        proj_group(ws[i][0], ws[i][1], specs[i][1], t[0], t[1])
                if pre is not None:
                    pre()
                    pre = None
                for i in range(2):
                    proj_group(ws[i][0], ws[i][1], specs[i][1], tl[-1][0], tl[-1][1])
                specs = specs[2:]
            if pre is not None:
                pre()
            for (wc, dst) in specs:
                wv_, bw = self.wload(W3[:, :, wc:wc + 128], KC, 128)
                for (t0, t1) in tl:
                    proj_group(wv_, bw, dst, t0, t1)

        self.kmch = [S.new_chan("kmch%d" % i) for i in range(2)]
        self.vmch = [S.new_chan("vmch%d" % i) for i in range(2)]
        kcch = [S.new_chan("kcch%d" % i) for i in range(2)]
        vcch = [S.new_chan("vcch%d" % i) for i in range(2)]
        b_KT = lambda c0, c1: bl("KT", [0], c0, c1)
        b_V = lambda blk: bb("V", blk)
        Ev6 = E[:].rearrange("p j (hk ge par) q -> p j hk ge par q", hk=4, ge=2)
        t16q = [self.sb("t16q%d" % i, [128, 128], BF16) for i in range(4)]
        t16qr = self.ring("t16q", 4)

        def t16q_next():
            i = self.nxt(t16qr)
            return t16q[i], bb("t16q", i)

        def pipeline(items):
            nst = max(len(it) for it in items)
            for step in range(len(items) + nst - 1):
                for s_ in range(nst):
                    i = step - s_
                    if 0 <= i < len(items) and s_ < len(items[i]) and items[i][s_] is not None:
                        items[i][s_]()

        def interleave(a, b):
            out = []
            nb_ = len(b)
            if nb_ == 0:
                return list(a)
            per = max(1, len(a) // nb_)
            bi = 0
            for i, it in enumerate(a):
                out.append(it)
                if (i + 1) % per == 0 and bi < nb_:
                    out.append(b[bi])
                    bi += 1
            out.extend(b[bi:])
            return out

        def mem_prefetch(l, p, s):
            i2 = s % 2
            self.dma("pool", kmc[i2][:], cmk[l, p, s].rearrange("(c m) n -> m c n", m=128), [], [bb("kmc", i2)], self.kmch[i2])
            self.dma("pool", vmc[i2][:], cmv[l, p, s].rearrange("(c m) n -> m c n", m=128), [], [bb("vmc", i2)], self.vmch[i2])

        def swa_prefetch(p, s):
            i2 = s % 2
            src = ck[p, s].rearrange("k (a d) -> k a d", a=4)
            for dup in range(2):
                self.dma("pool", kc16[i2][:, :, dup, :], src, [], [bb("kc16", i2)], kcch[i2], join=(dup == 1))
            self.dma("pool", vc16[i2][:], cv[p, s], [], [bb("vc16", i2)], vcch[i2])

        def mem_attn(l, p, c0):
            sc = 128 ** -0.5
            b_mkt, b_mv = bb("mKT"), bb("mV")
            b_msc = bb("mscr", l)
            self.dma("sp", mKT[:].rearrange("p a b -> p (a b)"), mkt_s[l], [b_msc], [b_mkt], mchan())
            self.dma("sp", mV[:], mv_s[l].rearrange("c p n -> p c n"), [b_msc], [b_mv], mchan())
            qs = bl("big", range(8, 12), CS0, NT)

            def prompt_item(t0, t1, hh):
                n = t1 - t0
                qb = bl("big", [8 + hh], t0, t1)
                ctx = {}

                def s0():
                    pts = []
                    for mc in range(2):
                        s_, bs_ = self.ps_next()
                        self.pe([(s_[:, 0:n], mKT[:, hh, mc * 128:(mc + 1) * 128], big[:, 8 + hh, t0:t1], True, True)], qb + [b_mkt], [bs_])
                        pp, bpp = self.t16_next()
                        self.act(pp[:, 0:n], s_[:, 0:n], AF.Exp, [bs_], [bpp], scale=sc)
                        pts.append((pp, bpp))
                    ctx["pts"] = pts

                def s1():
                    pts = ctx["pts"]
                    pa, bpa = self.ps_next()
                    self.pe([(pa[:, 0:n], mV[:, mc, hh * 128:(hh + 1) * 128], pts[mc][0][:, 0:n], mc == 0, mc == 1) for mc in range(2)],
                            [b_mv, pts[0][1], pts[1][1]], [bpa])
                    pb, bpb = self.ps_next()
                    self.pe([(pb[:, 0:n], ones[:], pts[mc][0][:, 0:n], mc == 0, mc == 1) for mc in range(2)],
                            [b_ones, pts[0][1], pts[1][1]], [bpb])
                    rb_, brb = self.t32_next()
                    self.act(rb_[:, 0:n], pb[:, 0:n], AF.Ln, [bpb], [brb])
                    self.act(rb_[:, 0:n], rb_[:, 0:n], AF.Exp, [brb], [brb], scale=-1.0)
                    self.tt(big[:, 8 + hh, t0:t1], pa[:, 0:n], rb_[:, 0:n], ALU.mult, [bpa, brb], qb)
                return [s0, s1]

            def seq_item(s):
                i2 = s % 2
                b_km, b_vm, b_kT = bb("kmc", i2), bb("vmc", i2), bb("kmT")
                cq0 = CS0 + s * 8
                ctx = {}

                def r1():
                    pt, bpt = self.ps_next()
                    ptb = pt.bitcast(BF16)
                    self.pe_tr([(ptb[:, (hh * 2 + mc) * 128:(hh * 2 + mc + 1) * 128], kmc[i2][:, mc, hh * 128:(hh + 1) * 128], identb[:])
                                for hh in range(4) for mc in range(2)], [b_km, b_identb], [bpt])
                    self.cp(kmT[:].rearrange("p a b c -> p (a b c)"), ptb[:, 0:1024], [bpt], [b_kT])

                def r2():
                    s_, bs_ = self.ps_next()
                    self.pe([(s_[:, (hh * 2 + mc) * 8:(hh * 2 + mc + 1) * 8], kmT[:, hh, mc, :], big[:, 8 + hh, cq0:cq0 + 8], True, True)
                             for hh in range(4) for mc in range(2)], qs + [b_kT], [bs_])
                    pp, bpp = t16q_next()
                    self.act(pp[:, 0:64], s_[:, 0:64], AF.Exp, [bs_], [bpp], scale=sc)
                    ctx["pp"] = (pp, bpp)

                def r3():
                    pp, bpp = ctx["pp"]
                    first = (s == 0)
                    lst = []
                    for hh in range(4):
                        for mc in range(2):
                            lst.append((ps6[:, s * 32 + hh * 8: s * 32 + hh * 8 + 8], vmc[i2][:, mc, hh * 128:(hh + 1) * 128],
                                        pp[:, (hh * 2 + mc) * 8:(hh * 2 + mc + 1) * 8], first and hh == 0 and mc == 0,
                                        s == NSEQ - 1 and hh == 3 and mc == 1))
                    self.pe(lst, [b_vm, bpp], [bps6])
                    ppv = pp[:, 0:64].rearrange("p (h c i) -> p h c i", h=4, c=2)
                    self.pe([(ps7[:, s * 32:(s + 1) * 32].rearrange("p (h i) -> p h i", h=4), ones[:], ppv[:, :, mc, :], first and mc == 0, s == NSEQ - 1 and mc == 1)
                             for mc in range(2)], [b_ones, bpp], [bps7])
                    if s + 2 < NSEQ:
                        mem_prefetch(l, p, s + 2)
                return [r1, r2, r3]

            def fin():
                rb_, brb = self.t32_next()
                self.act(rb_[:, 0:256], ps7[:, 0:256], AF.Ln, [bps7], [brb])
                self.act(rb_[:, 0:256], rb_[:, 0:256], AF.Exp, [brb], [brb], scale=-1.0)
                self.tt(big[:, 8:12, CS0:NT].rearrange("p h (s i) -> p h s i", s=NSEQ),
                        ps6[:, 0:256].rearrange("p (s h i) -> p h s i", s=NSEQ, h=4),
                        rb_[:, 0:256].rearrange("p (s h i) -> p h s i", s=NSEQ, h=4), ALU.mult, [bps6, brb], qs)

            pit = [prompt_item(t0, t1, hh) for (t0, t1) in tiles(c0, CS0) for hh in range(4)]
            sit = [seq_item(s) for s in range(NSEQ)]
            pipeline(interleave(pit, sit) + [[None, None, fin]])

        def swa_attn(jl, p):
            sc = 64 ** -0.5
            qs = bl("big", range(8), CS0, NT)
            ktn = b_KT(CS0, NT)

            def prompt_item(nb, hk):
                cq0 = CM0 + 128 * nb
                cq1 = cq0 + 128
                kcols = [(4, 132) if nb == 0 else (cq0 - 128, cq0), (cq0, cq1)]
                vblk = [nb, nb + 1]
                qb = bl("big", [2 * hk, 2 * hk + 1], cq0, cq1)
                ctx = {}

                def s0():
                    PT = []
                    for par in range(2):
                        s_, bs_ = self.ps_next()
                        pr = slice(par * 64, (par + 1) * 64)
                        self.pe([(s_[:, j * 256:(j + 1) * 256].rearrange("p (g q) -> p g q", g=2),
                                  KT[pr, hk, kcols[j][0] - 4:kcols[j][1] - 4], big[pr, 2 * hk:2 * hk + 2, cq0:cq1], True, True) for j in range(2)],
                                qb + b_KT(kcols[0][0], kcols[0][1]) + b_KT(cq0, cq1), [bs_])
                        pt_, bpt_ = self.t16_next()
                        self.act(pt_[:], s_[:, 0:512], AF.Exp, [bs_], [bpt_], scale=sc)
                        self.tt(pt_[:].rearrange("p (j g q) -> p j g q", j=2, g=2), pt_[:].rearrange("p (j g q) -> p j g q", j=2, g=2),
                                Ev6[:, :, hk, :, par, :], ALU.mult, [bpt_, b_E], [bpt_])
                        PT.append((pt_, bpt_))
                    ctx["PT"] = PT

                def s1():
                    PT = ctx["PT"]
                    pa, bpa = self.ps_next()
                    pb, bpb = self.ps_next()
                    la, lb_ = [], []
                    for par in range(2):
                        pr = slice(par * 64, (par + 1) * 64)
                        for j in range(2):
                            la.append((pa[pr, 0:256], V[:, vblk[j], hk * 64:(hk + 1) * 64], PT[par][0][:, j * 256:(j + 1) * 256], j == 0, j == 1))
                            on = hvt[:, 0:64] if (nb == 0 and j == 0) else ones[:, 0:64]
                            lb_.append((pb[pr, 0:256], on, PT[par][0][:, j * 256:(j + 1) * 256], j == 0, j == 1))
                    self.pe(la, [b_V(vblk[0]), b_V(vblk[1]), PT[0][1], PT[1][1]], [bpa])
                    self.pe(lb_, [b_ones, b_hv, PT[0][1], PT[1][1]], [bpb])
                    rb_, brb = self.t32_next()
                    for ge in range(2):
                        col = jl * 8 + hk * 2 + ge
                        self.act(rb_[:, ge * 128:(ge + 1) * 128], pb[:, ge * 128:(ge + 1) * 128], AF.Ln, [bpb, b_esb], [brb], bias=esb[:, col:col + 1])
                    self.act(rb_[:, 0:256], rb_[:, 0:256], AF.Exp, [brb], [brb], scale=-1.0)
                    self.tt(big[:, 2 * hk:2 * hk + 2, cq0:cq1], pa[:, 0:256].rearrange("p (g q) -> p g q", g=2),
                            rb_[:, 0:256].rearrange("p (g q) -> p g q", g=2), ALU.mult, [bpa, brb], qb)
                return [s0, s1]

            def new_item():
                ctx = {}

                def n0():
                    PTn = []
                    for par in range(2):
                        pr = slice(par * 64, (par + 1) * 64)
                        s_, bs_ = self.ps_next()
                        self.pe([(s_[0:64, hk * 128:(hk + 1) * 128].rearrange("p (g q) -> p g q", g=2), KT[pr, hk, CS0 - 4:NT - 4],
                                  big[pr, 2 * hk:2 * hk + 2, CS0:NT], True, True) for hk in range(4)], qs + ktn, [bs_])
                        pt_, bpt_ = self.t16_next()
                        self.act(pt_[0:64, :], s_[0:64, 0:512], AF.Exp, [bs_], [bpt_], scale=sc)
                        self.tt(pt_[0:64, :], pt_[0:64, :], En[:, par, :], ALU.mult, [bpt_, b_En], [bpt_])
                        PTn.append((pt_, bpt_))
                    ctx["PTn"] = PTn

                def n1():
                    PTn = ctx["PTn"]
                    la, lb_ = [], []
                    for par in range(2):
                        pr = slice(par * 64, (par + 1) * 64)
                        for hk in range(4):
                            la.append((ps6[pr, hk * 128:(hk + 1) * 128], V[0:64, 9, hk * 64:(hk + 1) * 64], PTn[par][0][0:64, hk * 128:(hk + 1) * 128], hk == 0, False))
                        lb_.append((ps7[pr, 0:512], ones[0:64, 0:64], PTn[par][0][0:64, 0:512], True, False))
                    self.pe(la, [b_V(9), PTn[0][1], PTn[1][1]], [bps6])
                    self.pe(lb_, [b_ones, PTn[0][1], PTn[1][1]], [bps7])
                return [n0, n1]

            def seq_item(s):
                i2 = s % 2
                b_kc, b_kT, b_vc = bb("kc16", i2), bb("kcT", i2), bb("vc16", i2)
                cq0 = CS0 + s * 8
                ctx = {}

                def q1():
                    pt, bpt = self.ps_next()
                    ptb = pt.bitcast(BF16)
                    self.pe_tr([(ptb[:, hk * 128:(hk + 1) * 128], kc16[i2][:, hk, :, :].rearrange("p a d -> p (a d)"), identb[:]) for hk in range(4)],
                               [b_kc, b_identb], [bpt])
                    self.cp(kcT[i2][:].rearrange("p a k -> p (a k)"), ptb[:, 0:512], [bpt], [b_kT], eng="act")

                def q2():
                    PTc = []
                    for par in range(2):
                        pr = slice(par * 64, (par + 1) * 64)
                        s_, bs_ = self.ps_next()
                        self.pe([(s_[:, hk * 32:(hk + 1) * 32].rearrange("p (g c) -> p g c", g=2)[:, :, 0:8], kcT[i2][pr, hk, :],
                                  big[pr, 2 * hk:2 * hk + 2, cq0:cq0 + 8], True, True)
                                 for hk in range(4)], qs + [b_kT], [bs_])
                        pt_, bpt_ = t16q_next()
                        pv8 = pt_[:, 0:128].rearrange("p (a c) -> p a c", a=8)[:, :, 0:8]
                        self.act(pv8, s_[:, 0:128].rearrange("p (a c) -> p a c", a=8)[:, :, 0:8], AF.Exp, [bs_], [bpt_], scale=sc)
                        self.tt(pv8, pv8, Ev[:, 0, :, par, 0:8], ALU.mult, [bpt_, b_E], [bpt_])
                        PTc.append((pt_, bpt_))
                    ctx["PTc"] = PTc

                def q3():
                    PTc = ctx["PTc"]
                    la, lb_ = [], []
                    for par in range(2):
                        pr = slice(par * 64, (par + 1) * 64)
                        for hk in range(4):
                            la.append((ps6[pr, hk * 128:(hk + 1) * 128].rearrange("p (g s i) -> p g s i", g=2, s=NSEQ)[:, :, s, :],
                                       vc16[i2][:, hk * 64:(hk + 1) * 64], PTc[par][0][:, hk * 32:(hk + 1) * 32].rearrange("p (g c) -> p g c", g=2)[:, :, 0:8],
                                       False, s == NSEQ - 1 and hk == 3))
                        lb_.append((ps7[pr, 0:512].rearrange("p (a s i) -> p a s i", a=8, s=NSEQ)[:, :, s, :], ones[:, 0:64],
                                    PTc[par][0][:, 0:128].rearrange("p (a c) -> p a c", a=8)[:, :, 0:8], False, s == NSEQ - 1))
                    self.pe(la, [b_vc, PTc[0][1], PTc[1][1]], [bps6])
                    self.pe(lb_, [b_ones, PTc[0][1], PTc[1][1]], [bps7])
                    if s + 2 < NSEQ:
                        swa_prefetch(p, s + 2)
                return [q1, q2, q3]

            def fin():
                rb_, brb = self.t32_next()
                for a in range(8):
                    col = jl * 8 + a
                    self.act(rb_[:, a * 64:(a + 1) * 64], ps7[:, a * 64:(a + 1) * 64], AF.Ln, [bps7, b_esb], [brb], bias=esb[:, col:col + 1])
                self.act(rb_[:], rb_[:], AF.Exp, [brb], [brb], scale=-1.0)
                self.tt(big[:, 0:8, CS0:NT], ps6[:, 0:512].rearrange("p (a q) -> p a q", a=8), rb_[:].rearrange("p (a q) -> p a q", a=8),
                        ALU.mult, [bps6, brb], qs)

            pit = [prompt_item(nb, hk) for nb in range(8) for hk in range(4)]
            sit = [seq_item(s) for s in range(NSEQ)]
            pipeline([new_item()] + interleave(pit, sit) + [[None, None, fin]])

        for p in range(NPASS):
            self.dma("pool", hvt[:], hv_d[p], [], [b_hv], pchan())
            self.dma("sp", hvf[:], hv_d[p][:, 0:1], [], [bb("hvf")], mchan(), allow_slow_non_contiguous=True)
            b_scp = bb("scp")
            for l in range(2):
                sg, bsg, chn = self.stg_next()
                self.dma("sp", sg[0:16, :], scv[p, l], [], [bsg], chn)
                for half in range(2):
                    pt, bpt = self.ps_next()
                    self.pe_tr([(pt[:, i * 16:(i + 1) * 16], sg[0:16, (half * 4 + i) * 128:(half * 4 + i + 1) * 128], ident[0:16, 0:16]) for i in range(4)],
                               [bsg, b_ident], [bpt])
                    self.cp(scp[:, l, half * 4:half * 4 + 4, :], pt[:, 0:64].rearrange("p (c t) -> p c t", c=4), [bpt], [b_scp])
            self.stage("p%da" % p)
            self.stage("p%db" % p)
            if p == 0:
                xblocks = [(i * 128, (i + 1) * 128) for i in range(9)] + [(1152, 1216), (1216, NT)]
            else:
                xblocks = [(CM0 + i * 128, CM0 + (i + 1) * 128) for i in range(8)] + [(CS0, NT)]
            n0_tiles = tiles(0 if p == 0 else CM0, NT)
            n0_done = 0
            for (r0, r1) in xblocks:
                n = r1 - r0
                sg, bsg, chn = self.stg_next()
                self.dma("sp", sg[0:n, :], xin[p, r0:r1, :], [], [bsg], chn)
                for half in range(2):
                    pt, bpt = self.ps_next()
                    self.pe_tr([(pt[:, i * 128:i * 128 + n], sg[0:n, (half * 4 + i) * 128:(half * 4 + i + 1) * 128], ident[0:n, 0:n]) for i in range(4)],
                               [bsg, b_ident], [bpt])
                    self.cp(x[:, half * 4:half * 4 + 4, r0:r1], pt[:, 0:512].rearrange("p (c t) -> p c t", c=4)[:, :, 0:n],
                            [bpt], bl("x", range(half * 4, half * 4 + 4), r0, r1), eng=("act" if half == 0 else "dve"))
                while n0_done < len(n0_tiles) and n0_tiles[n0_done][1] <= r1:
                    norm_sq(n0_tiles[n0_done][0], n0_tiles[n0_done][1])
                    norm_rest(0, n0_tiles[n0_done][0], n0_tiles[n0_done][1])
                    n0_done += 1

            self.dma("sp", sks[p][:, 0:120, :], ck[p][:, 8:128, :], [], [], mchan())
            self.dma("sp", svs[p][:, 0:120, :], cv[p][:, 8:128, :], [], [], mchan())
            self.stage("p%dload" % p)
            dfr = None
            for l in range(DEPTH):
                c0 = 0 if (l < 2 and p == 0) else CM0
                if l == 2:
                    if dfr is not None:
                        dfr()
                        dfr = None
                    Wk3 = w_kv.rearrange("(k p) n -> p k n", p=128)
                    for hp in range(2):
                        i = self.nxt(self.wr)
                        wkd = self.wt[i][:, 0:KC * 256].rearrange("p (k a u d) -> p k a u d", k=KC, a=2, u=2)
                        bw = bb("w", i)
                        for dup in range(2):
                            for a_ in range(2):
                                self.dma("pool", wkd[:, :, a_, dup, :], Wk3[:, :, hp * 128 + a_ * 64:hp * 128 + (a_ + 1) * 64],
                                         [], [bw], self.wc[i], join=(dup + a_ > 0))
                        for a_ in range(2):
                            hk = hp * 2 + a_
                            if p == 1 and a_ == 0 and hp == 0:
                                self.cp(KT[:, :, 0:128], KT[:, :, 1024:1152], b_KT(1028, 1156), b_KT(4, 132))
                                self.cp(V[:, 0, :], V[:, 8, :], [b_V(8)], [b_V(0)])
                            for (t0, t1) in tiles(4 if p == 0 else CM0, NT):
                                n = t1 - t0
                                pt, bpt = self.ps_next()
                                self.pe([(pt[:, 0:n], wkd[:, k, a_, :, :].rearrange("p u d -> p (u d)"), h[:, k, t0:t1], k == 0, k == KC - 1) for k in range(KC)],
                                        bl("h", range(KC), t0, t1) + [bw], [bpt])
                                self.cp(KT[:, hk, t0 - 4:t1 - 4], pt[:, 0:n], [bpt], b_KT(t0, t1), eng="act")
                    wkt, bwk = self.wload(Wk3[:, :, 0:256], KC, 256)
                    wvt, bwv = self.wload(Wk3[:, :, 256:512], KC, 256)
                    for blk in range(0 if p == 0 else 1, 10):
                        r0, r1 = ABND[blk], ABND[blk + 1]
                        n = r1 - r0
                        hb = bl("h", range(KC), r0, r1)
                        pv, bpv = self.ps_next()
                        self.pe([(pv[0:n, 0:256], h[:, k, r0:r1], wvt[:, k, :], k == 0, k == KC - 1) for k in range(KC)], hb + [bwv], [bpv])
                        self.cp(V[0:n, blk, :], pv[0:n, 0:256], [bpv], [b_V(blk)], eng="act")
                        if blk >= 8:
                            pk, bpk = self.ps_next()
                            self.pe([(pk[0:n, 0:256], h[:, k, r0:r1], wkt[:, k, :], k == 0, k == KC - 1) for k in range(KC)], hb + [bwk], [bpk])
                            sg, bsg, chn = self.stg_next()
                            self.cp(sg[0:n, 0:256], pk[0:n, 0:256], [bpk], [bsg])
                            self.cp(sg[0:n, 256:512], pv[0:n, 0:256], [bpv], [bsg])
                            if blk == 8:
                                self.dma("sp", swk[p], sg[:, 0:256], [bsg], [], chn)
                                self.dma("sp", swv[p], sg[:, 256:512], [bsg], [], chn)
                            else:
                                for s_i in range(NSEQ):
                                    self.dma("sp", sks[p][s_i, 120:128, :], sg[s_i * 8:(s_i + 1) * 8, 0:256], [bsg], [], chn, join=(s_i > 0))
                                    self.dma("sp", svs[p][s_i, 120:128, :], sg[s_i * 8:(s_i + 1) * 8, 256:512], [bsg], [], chn, join=True)

                self.stage("p%dl%dkv" % (p, l))
                dfr = ffn(w_f1g[l], w_f1u[l], w_f1d[l], 0 + l, c0, NT, skip_norm=(l != 2), nxt=(4 + l, c0, NT, False), pre=dfr)
                for s_pf in range(2):
                    mem_prefetch(l, p, s_pf)
                    if l >= 2:
                        swa_prefetch(p, s_pf)
                self.stage("p%dl%df1" % (p, l))
                if l < 2:
                    Wi3 = w_ina[l].rearrange("(k p) n -> p k n", p=128)
                    proj_to_big(Wi3, [(3072 + hh * 128, 8 + hh) for hh in range(4)], c0, NT, pre=dfr)
                    dfr = None
                    temps = [b_tmpc, b_acc, b_up, b_us, b_cs]
                    S.op("dve", lambda e: e.memset(up[:, 0:2], 0.0), [], allbig(range(12, 19)) + temps)
                    tl = tiles(c0, NT)
                    for i in range(KC):
                        wv_, bw = self.wload(Wi3[:, :, 1024 + i * 128:1024 + (i + 1) * 128], KC, 128)
                        for (t0, t1) in tl:
                            n = t1 - t0
                            pt, bpt = self.ps_next()
                            self.pe([(pt[:, 0:n], wv_[:, k, :], h[:, k, t0:t1], k == 0, k == KC - 1) for k in range(KC)], bl("h", range(KC), t0, t1) + [bw], [bpt])
                            self.cp(tmpc[:, t0:t1], pt[:, 0:n], [bpt], [b_tmpc], eng="act")
                        self.cp(us[:, :, 0:2], scp[:, l, i, :].rearrange("p (s r) -> p s r", s=NSEQ), [b_scp], [b_us])
                        if p == 1:
                            self.cp(up[:, c0:c0 + 2], uprev[:, l, i, :], [bb("uprev")], [b_up])
                        wv_, bw = self.wload(Wi3[:, :, 2048 + i * 128:2048 + (i + 1) * 128], KC, 128)
                        for (t0, t1) in tl:
                            n = t1 - t0
                            pt, bpt = self.ps_next()
                            self.pe([(pt[:, 0:n], wv_[:, k, :], h[:, k, t0:t1], k == 0, k == KC - 1) for k in range(KC)], bl("h", range(KC), t0, t1) + [bw], [bpt])
                            e1 = min(t1, CS0)
                            if t0 < e1:
                                self.tt(up[:, 2 + t0:2 + e1], tmpc[:, t0:e1], pt[:, 0:e1 - t0], ALU.mult, [bpt, b_tmpc], [b_up])
                            if t1 > CS0:
                                s0 = max(t0, CS0)
                                self.tt(us[:, :, 2:10], tmpc[:, s0:t1].rearrange("p (s i) -> p s i", s=NSEQ),
                                        pt[:, s0 - t0:t1 - t0].rearrange("p (s i) -> p s i", s=NSEQ), ALU.mult, [bpt, b_tmpc], [b_us])
                        w0, w1, w2 = G(18 + l * 3 + 0, i), G(18 + l * 3 + 1, i), G(18 + l * 3 + 2, i)
                        self.ts(acc[:, c0:CS0], up[:, 2 + c0:2 + CS0], w2, ALU.mult, [b_up, b_gains], [b_acc])
                        self.stt(acc[:, c0:CS0], up[:, 1 + c0:1 + CS0], w1, acc[:, c0:CS0], ALU.mult, ALU.add, [b_up, b_acc, b_gains], [b_acc])
                        self.stt(acc[:, c0:CS0], up[:, c0:CS0], w0, acc[:, c0:CS0], ALU.mult, ALU.add, [b_up, b_acc, b_gains], [b_acc])
                        accs = acc[:, CS0:NT].rearrange("p (s i) -> p s i", s=NSEQ)
                        self.ts(accs, us[:, :, 2:10], w2, ALU.mult, [b_us, b_gains], [b_acc])
                        self.stt(accs, us[:, :, 1:9], w1, accs, ALU.mult, ALU.add, [b_us, b_acc, b_gains], [b_acc])
                        self.stt(accs, us[:, :, 0:8], w0, accs, ALU.mult, ALU.add, [b_us, b_acc, b_gains], [b_acc])
                        self.cp(cs[:, i, 0:16].rearrange("p (s r) -> p s r", s=NSEQ), us[:, :, 8:10], [b_us], [b_cs])
                        self.cp(cs[:, i, 16:18], up[:, CS0:CS0 + 2], [b_up], [b_cs])
                        if p == 0:
                            self.cp(uprev[:, l, i, :], up[:, CS0:CS0 + 2], [b_up], [bb("uprev")])
                        wv_, bw = self.wload(Wi3[:, :, i * 128:(i + 1) * 128], KC, 128)
                        for (t0, t1) in tl:
                            n = t1 - t0
                            pt, bpt = self.ps_next()
                            self.pe([(pt[:, 0:n], wv_[:, k, :], h[:, k, t0:t1], k == 0, k == KC - 1) for k in range(KC)], bl("h", range(KC), t0, t1) + [bw], [bpt])
                            self.tt(big[:, i, t0:t1], acc[:, t0:t1], pt[:, 0:n], ALU.mult, [bpt, b_acc], bl("big", [i], t0, t1))
                    sg, bsg, chn = self.stg_next()
                    for half in range(2):
                        pt, bpt = self.ps_next()
                        self.pe_tr([(pt[0:18, i * 128:(i + 1) * 128], cs[:, half * 4 + i, :], ident[:]) for i in range(4)], [b_cs, b_ident], [bpt])
                        self.cp(sg[0:18, half * 512:(half + 1) * 512], pt[0:18, 0:512], [bpt], [bsg])
                    self.dma("sp", cso[p, l], sg[0:18, :], [bsg], [], chn)
                    S.op("dve", lambda e: e.memset(up[:, 0:1], 0.0), temps, allbig(range(12, 19)) + temps)
                    mem_attn(l, p, c0)
                    dfr = wout(w_outa[l], c0, NT, nxt=(8 + l, c0, NT, False), halo_mask=(p == 0))
                else:
                    jl = l - 2
                    Wi3 = w_inb[jl].rearrange("(k p) n -> p k n", p=128)
                    proj_to_big(Wi3, [(c * 128, c) for c in range(12)], CM0, NT, pre=dfr)
                    dfr = None
                    swa_attn(jl, p)
                    mem_attn(l, p, CM0)
                    dfr = wout(w_outb[jl], CM0, NT, nxt=(8 + l, CM0, NT, False))
                self.stage("p%dl%dmix" % (p, l))
                if l == 0:
                    nx = (1, c0, NT, False)
                elif l == 1:
                    nx = (16, c0, NT, False)
                elif l == 2:
                    nx = (3, CM0, NT, False)
                else:
                    nx = (17, CM0, NT, True)
                dfr = ffn(w_f2g[l], w_f2u[l], w_f2d[l], 8 + l, c0, NT, skip_norm=True, nxt=nx, pre=dfr)
                if l == DEPTH - 1 and dfr is not None:
                    dfr()
                    dfr = None
                self.stage("p%dl%df2" % (p, l))

            for (r0, r1) in [(CM0 + i * 128, min(CM0 + (i + 1) * 128, NT)) for i in range(9)]:
                n = r1 - r0
                sg, bsg, chn = self.stg_next()
                for half in range(2):
                    pt, bpt = self.ps_next()
                    self.pe_tr([(pt[0:n, i * 128:(i + 1) * 128], x[:, half * 4 + i, r0:r1], ident[:]) for i in range(4)],
                               bl("x", range(half * 4, half * 4 + 4), r0, r1) + [b_ident], [bpt])
                    self.cp(sg[0:n, half * 512:(half + 1) * 512], pt[0:n, 0:512], [bpt], [bsg], eng=("act" if half == 0 else "dve"))
                self.dma("sp", yout[p, r0 - CM0:r1 - CM0, :], sg[0:n, :], [bsg], [], chn)


def _bucket_onehot():
    d = np.arange(128)
    exact = 16
    nf = np.maximum(d, 1).astype(np.float32)
    large = exact + (np.log(nf / np.float32(exact)) / np.float32(math.log(128 / exact)) * np.float32(32 - exact)).astype(np.int32)
    large = np.minimum(large, 31)
    bk = np.where(d < exact, d, large)
    oh = np.zeros((32, 128), np.float32)
    oh[bk, d] = 1.0
    return oh


_NC_CACHE = {}


def kernel(x_prompt, x_sample, state_conv, cache_swa_k, cache_swa_v, cache_mem_k, cache_mem_v,
           mem_prompt, ffn1_norm, ffn1_wg, ffn1_wu, ffn1_wd, mix_norm, w_in_a, conv_w, w_out_a,
           kv_norm, w_kv, w_in_b, attn_sinks, rel_bias, w_out_b, mem_norm, w_mem_kv,
           ffn2_norm, ffn2_wg, ffn2_wu, ffn2_wd, final_norm):
    f = lambda a: np.ascontiguousarray(np.asarray(a, dtype=np.float32))
    x_prompt, x_sample, state_conv = f(x_prompt), f(x_sample), f(state_conv)
    cache_swa_k, cache_swa_v = f(cache_swa_k), f(cache_swa_v)
    cache_mem_k, cache_mem_v, mem_prompt = f(cache_mem_k), f(cache_mem_v), f(mem_prompt)
    if "nc" not in _NC_CACHE:
        _NC_CACHE["nc"] = Prog().build()
    nc = _NC_CACHE["nc"]
    in_maps = _make_in_maps(x_prompt, x_sample, state_conv, cache_swa_k, cache_swa_v, cache_mem_k, cache_mem_v,
                            mem_prompt, ffn1_norm, ffn1_wg, ffn1_wu, ffn1_wd, mix_norm, w_in_a, conv_w, w_out_a,
                            kv_norm, w_kv, w_in_b, attn_sinks, rel_bias, w_out_b, mem_norm, w_mem_kv,
                            ffn2_norm, ffn2_wg, ffn2_wu, ffn2_wd, final_norm)
    res = run_bass_kernel_spmd(nc, in_maps, core_ids=list(range(8)))
    return _gather(res.results)


def _make_in_maps(x_prompt, x_sample, state_conv, cache_swa_k, cache_swa_v, cache_mem_k, cache_mem_v,
                  mem_prompt, ffn1_norm, ffn1_wg, ffn1_wu, ffn1_wd, mix_norm, w_in_a, conv_w, w_out_a,
                  kv_norm, w_kv, w_in_b, attn_sinks, rel_bias, w_out_b, mem_norm, w_mem_kv,
                  ffn2_norm, ffn2_wg, ffn2_wu, ffn2_wd, final_norm, cores=range(8)):
    f = lambda a: np.ascontiguousarray(np.asarray(a, dtype=np.float32))

    vecs = np.concatenate([f(ffn1_norm), f(mix_norm), f(ffn2_norm), f(mem_norm), f(kv_norm)[None], f(final_norm)[None],
                           f(conv_w).reshape(6, D)], axis=0)
    sk = f(attn_sinks).reshape(2, 4, 2, 2)
    sinks = np.ascontiguousarray(sk.transpose(3, 0, 1, 2).reshape(2, 16))
    ident = np.eye(128, dtype=np.float32)
    jm = np.zeros((128, 383), np.float32)
    dd = np.arange(128)
    jm[dd, 255 - dd] = 1.0
    oh = _bucket_onehot()
    bd = np.kron(np.eye(8, dtype=np.float32), np.ones((8, 8), np.float32))
    shared = dict(vecs=vecs, f1g=f(ffn1_wg), f1u=f(ffn1_wu), f1d=f(ffn1_wd), f2g=f(ffn2_wg), f2u=f(ffn2_wu), f2d=f(ffn2_wd),
                  wina=f(w_in_a), wouta=f(w_out_a), winb=f(w_in_b), woutb=f(w_out_b), wkv=f(w_kv), wmkv=f(w_mem_kv),
                  relb=f(rel_bias), sinks=sinks, ident=ident, jmat=jm, ohot=oh, bdiag=bd)
    in_maps = []
    for c in cores:
        b, q = c // 4, c % 4
        xin = np.zeros((NPASS, NT, D), np.float32)
        hv = np.zeros((NPASS, 128, 64), np.float32)
        for p in range(NPASS):
            t0 = q * 2048 + p * MAIN
            lo = t0 - HALO
            if lo >= 0:
                xin[p, 0:HALO + MAIN] = x_prompt[b, lo:t0 + MAIN]
                hv[p] = 1.0
            else:
                xin[p, HALO:HALO + MAIN] = x_prompt[b, t0:t0 + MAIN]
            s0 = c * 16 + p * NSEQ
            xin[p, CS0:NT] = x_sample[s0:s0 + NSEQ].reshape(SAMP, D)
        seqs = [slice(c * 16 + p * NSEQ, c * 16 + (p + 1) * NSEQ) for p in range(NPASS)]
        m = dict(shared)
        m["xin"] = xin
        m["hv"] = hv
        m["scv"] = np.stack([state_conv[:, s].reshape(2, 2 * NSEQ, D) for s in seqs], axis=0)
        m["ck"] = np.stack([cache_swa_k[s].reshape(NSEQ, 128, 256) for s in seqs], axis=0)
        m["cv"] = np.stack([cache_swa_v[s].reshape(NSEQ, 128, 256) for s in seqs], axis=0)
        m["cmk"] = np.stack([cache_mem_k[:, s].reshape(DEPTH, NSEQ, 256, 512) for s in seqs], axis=1)
        m["cmv"] = np.stack([cache_mem_v[:, s].reshape(DEPTH, NSEQ, 256, 512) for s in seqs], axis=1)
        m["mem"] = mem_prompt[b]
        in_maps.append({k: np.ascontiguousarray(v) for k, v in m.items()})
    return in_maps


def _gather(R):
    y_prompt = np.zeros((2, 8192, D), np.float32)
    y_sample = np.zeros((128, 8, D), np.float32)
    csp = np.zeros((2, 2, 2, D), np.float32)
    css = np.zeros((2, 128, 2, D), np.float32)
    swkp = np.zeros((2, 128, 4, 64), np.float32)
    swvp = np.zeros((2, 128, 4, 64), np.float32)
    swks = np.zeros((128, 128, 4, 64), np.float32)
    swvs = np.zeros((128, 128, 4, 64), np.float32)
    mkp = np.zeros((DEPTH, 2, 256, 4, 128), np.float32)
    mvp = np.zeros((DEPTH, 2, 256, 4, 128), np.float32)
    for c in range(8):
        b, q = c // 4, c % 4
        r = R[c]
        for p in range(NPASS):
            t0 = q * 2048 + p * MAIN
            y_prompt[b, t0:t0 + MAIN] = r["yout"][p, 0:MAIN]
            s0 = c * 16 + p * NSEQ
            y_sample[s0:s0 + NSEQ] = r["yout"][p, MAIN:NB].reshape(NSEQ, 8, D)
            css[:, s0:s0 + NSEQ] = r["cso"][p, :, 0:16].reshape(2, NSEQ, 2, D)
            swks[s0:s0 + NSEQ] = r["sks"][p].reshape(NSEQ, 128, 4, 64)
            swvs[s0:s0 + NSEQ] = r["svs"][p].reshape(NSEQ, 128, 4, 64)
        if q == 3:
            csp[:, b] = r["cso"][1, :, 16:18]
            swkp[b] = r["swk"][1].reshape(128, 4, 64)
            swvp[b] = r["swv"][1].reshape(128, 4, 64)
        if q == 0:
            mkp[:, b] = r["mko"].reshape(DEPTH, 256, 4, 128)
            mvp[:, b] = r["mvo"].reshape(DEPTH, 256, 4, 128)
    return (y_prompt, y_sample, csp, css, swkp, swvp, swks, swvs, mkp, mvp)
```
